# Optimizing a Trainium2 kernel written in Bass

```python
import jax, jax.numpy as jnp
from jax import lax
import numpy as np

D_MODEL = 1024
BATCH = 16
SEQ = 2048
DEPTH = 4

CHUNK = 64
N_MIXERS = 2
N_CONV_LAYERS = (DEPTH + 1) // 2
N_SSD_LAYERS = DEPTH // 2
N_MOD = 6
EPS = 1e-6

SC_WIDTH = 3

M_EXPAND = 2
M_D_INNER = M_EXPAND * D_MODEL
M_HEAD_DIM = 64
M_N_HEADS = M_D_INNER // M_HEAD_DIM
M_N_GROUPS = 8
M_HEADS_PER_GROUP = M_N_HEADS // M_N_GROUPS
M_D_STATE = 128
M_CONV_WIDTH = 4
M_CONV_DIM = M_D_INNER + 2 * M_N_GROUPS * M_D_STATE
M_IN_DIM = M_D_INNER + M_CONV_DIM + M_N_HEADS
SSD_CHUNK = CHUNK

D_FF = -(-8 * D_MODEL // (3 * 256)) * 256

kernel_name = "hybrid_shortconv_ssd_streaming_trunk"


def rms_normalize(x):
    xf = x.astype(jnp.float32)
    xf = xf * lax.rsqrt(jnp.mean(xf * xf, axis=-1, keepdims=True) + EPS)
    return xf.astype(x.dtype)


def rmsnorm(x, g):
    return rms_normalize(x) * g


def causal_depthwise_conv(x, w):
    k_width = w.shape[0]
    s = x.shape[1]
    xp = jnp.pad(x, ((0, 0), (k_width - 1, 0), (0, 0)))
    y = w[0] * xp[:, 0:s]
    for k in range(1, k_width):
        y = y + w[k] * xp[:, k:k + s]
    return y


def short_conv_mixer(h, w_in, conv_w, w_out):
    b_gate, c_gate, v = jnp.split(h @ w_in, 3, axis=-1)
    u = causal_depthwise_conv(c_gate * v, conv_w)
    return (b_gate * u) @ w_out


def ssd_chunked(xh, dt, a, bm, cm):
    bsz, s, g, r, p = xh.shape
    n = bm.shape[-1]
    nc, l = s // SSD_CHUNK, SSD_CHUNK
    dtype = xh.dtype
    xdt = (xh * dt[..., None].astype(dtype)).reshape(bsz, nc, l, g, r, p)
    bm = bm.reshape(bsz, nc, l, g, n)
    cm = cm.reshape(bsz, nc, l, g, n)
    cs = jnp.cumsum((dt * a).reshape(bsz, nc, l, g, r), axis=2)
    causal = jnp.tril(jnp.ones((l, l), dtype=bool))[None, None, :, :, None, None]
    seg = cs[:, :, :, None] - cs[:, :, None]
    decay = jnp.exp(jnp.where(causal, seg, -jnp.inf)).astype(dtype)
    cb = jnp.einsum('bclgn,bcsgn->bclsg', cm, bm)
    y_diag = jnp.einsum('bclsgr,bcsgrp->bclgrp', cb[..., None] * decay, xdt)
    decay_to_end = jnp.exp(cs[:, :, -1:] - cs).astype(dtype)
    states = jnp.einsum('bclgn,bclgr,bclgrp->bcgrpn', bm, decay_to_end, xdt)
    chunk_decay = jnp.exp(cs[:, :, -1]).astype(dtype)

    def step(state, inp):
        st, dec = inp
        return state * dec[..., None, None] + st, state

    h0 = jnp.zeros((bsz, g, r, p, n), dtype)
    _, prev = lax.scan(step, h0, (jnp.moveaxis(states, 1, 0), jnp.moveaxis(chunk_decay, 1, 0)))
    prev = jnp.moveaxis(prev, 0, 1)
    y_off = jnp.einsum('bclgn,bcgrpn,bclgr->bclgrp', cm, prev, jnp.exp(cs).astype(dtype))
    return (y_diag + y_off).reshape(bsz, s, g, r, p)


def ssd_mixer(h, w_in, conv_w, conv_b, dt_bias, a_log, d_skip, norm_g, w_out):
    bsz, s, _ = h.shape
    g, r, p, n = M_N_GROUPS, M_HEADS_PER_GROUP, M_HEAD_DIM, M_D_STATE
    z, xbc, dt_raw = jnp.split(h @ w_in, [M_D_INNER, M_D_INNER + M_CONV_DIM], axis=-1)
    xbc = jax.nn.silu(causal_depthwise_conv(xbc, conv_w) + conv_b)
    xs, bm, cm = jnp.split(xbc, [M_D_INNER, M_D_INNER + g * n], axis=-1)
    xs = xs.reshape(bsz, s, g, r, p)
    bm = bm.reshape(bsz, s, g, n)
    cm = cm.reshape(bsz, s, g, n)
    dt = jax.nn.softplus(dt_raw.astype(jnp.float32) + dt_bias.astype(jnp.float32)).reshape(bsz, s, g, r)
    a = -jnp.exp(a_log.astype(jnp.float32)).reshape(g, r)
    y = ssd_chunked(xs, dt, a, bm, cm) + d_skip.reshape(g, r)[:, :, None] * xs
    y = y.reshape(bsz, s, M_D_INNER) * jax.nn.silu(z)
    y = rms_normalize(y.reshape(bsz, s, g, M_D_INNER // g)).reshape(bsz, s, M_D_INNER) * norm_g
    return y @ w_out


def swiglu_ffn(h, w_in, w_out):
    gate, up = jnp.split(h @ w_in, 2, axis=-1)
    return (jax.nn.silu(gate) * up) @ w_out


def setup_inputs(seed: int = 0) -> dict:
    key = jax.random.key(seed)
    ks = jax.random.split(key, 20)
    f32 = jnp.float32
    nrm = lambda k, shape, scale: jax.random.normal(k, shape, f32) * scale
    dt_init = jnp.exp(jax.random.uniform(ks[11], (N_SSD_LAYERS, M_N_HEADS), f32,
                                         np.log(1e-3), np.log(1e-1)))
    return {
        "x": nrm(ks[0], (BATCH, SEQ, D_MODEL), 1.0),
        "c": nrm(ks[1], (BATCH, D_MODEL), 1.0),
        "ada_w": nrm(ks[2], (D_MODEL, DEPTH * N_MOD * D_MODEL), 0.5 * D_MODEL ** -0.5),
        "ada_b": nrm(ks[3], (DEPTH * N_MOD * D_MODEL,), 0.02),
        "norm_g": 1.0 + nrm(ks[4], (DEPTH, 4, D_MODEL), 0.02),
        "a_w_in": nrm(ks[5], (N_CONV_LAYERS, D_MODEL, 3 * D_MODEL), D_MODEL ** -0.5),
        "a_conv_w": nrm(ks[6], (N_CONV_LAYERS, SC_WIDTH, D_MODEL), SC_WIDTH ** -0.5),
        "a_w_out": nrm(ks[7], (N_CONV_LAYERS, D_MODEL, D_MODEL), D_MODEL ** -0.5),
        "m_w_in": nrm(ks[8], (N_SSD_LAYERS, D_MODEL, M_IN_DIM), D_MODEL ** -0.5),
        "m_conv_w": nrm(ks[9], (N_SSD_LAYERS, M_CONV_WIDTH, M_CONV_DIM), M_CONV_WIDTH ** -0.5),
        "m_conv_b": nrm(ks[10], (N_SSD_LAYERS, M_CONV_DIM), 0.02),
        "m_dt_bias": dt_init + jnp.log(-jnp.expm1(-dt_init)),
        "m_a_log": jnp.log(jax.random.uniform(ks[12], (N_SSD_LAYERS, M_N_HEADS), f32, 1.0, 16.0)),
        "m_d": 1.0 + nrm(ks[13], (N_SSD_LAYERS, M_N_HEADS), 0.1),
        "m_norm_g": 1.0 + nrm(ks[14], (N_SSD_LAYERS, M_D_INNER), 0.02),
        "m_w_out": nrm(ks[15], (N_SSD_LAYERS, M_D_INNER, D_MODEL), M_D_INNER ** -0.5),
        "f_w_in": nrm(ks[16], (DEPTH, D_MODEL, 2 * D_FF), D_MODEL ** -0.5),
        "f_w_out": nrm(ks[17], (DEPTH, D_FF, D_MODEL), D_FF ** -0.5),
    }


def reference(x, c, ada_w, ada_b, norm_g, a_w_in, a_conv_w, a_w_out, m_w_in, m_conv_w,
              m_conv_b, m_dt_bias, m_a_log, m_d, m_norm_g, m_w_out, f_w_in, f_w_out):
    bsz = x.shape[0]
    mod = (jax.nn.silu(c) @ ada_w + ada_b).reshape(bsz, DEPTH, 2, 3, D_MODEL)
    for i in range(DEPTH):
        j = i // N_MIXERS
        shift, scale, gate = (mod[:, i, 0, k][:, None, :] for k in range(3))
        h = rmsnorm(x, norm_g[i, 0]) * (1.0 + scale) + shift
        if i % N_MIXERS == 0:
            y = short_conv_mixer(h, a_w_in[j], a_conv_w[j], a_w_out[j])
        else:
            y = ssd_mixer(h, m_w_in[j], m_conv_w[j], m_conv_b[j], m_dt_bias[j], m_a_log[j],
                          m_d[j], m_norm_g[j], m_w_out[j])
        x = x + gate * rmsnorm(y, norm_g[i, 1])
        shift, scale, gate = (mod[:, i, 1, k][:, None, :] for k in range(3))
        h = rmsnorm(x, norm_g[i, 2]) * (1.0 + scale) + shift
        y = swiglu_ffn(h, f_w_in[i], f_w_out[i])
        x = x + gate * rmsnorm(y, norm_g[i, 3])
    return x
```

```python
import numpy as np
import concourse.bass as bass
import concourse.mybir as mybir
from concourse.bass_utils import run_bass_kernel_spmd

F32 = mybir.dt.float32
BF16 = mybir.dt.bfloat16
AF = mybir.ActivationFunctionType
ALU = mybir.AluOpType

NCORES = 8
D = 1024
KD = 8
SEQ = 2048
DEPTH = 4
DFF = 2816
KF = 22
M_DI = 2048
M_IN = 6176
EPS = 1e-6
NSEQ = 2
NT = 2
TU = 512 * NT
UPS = SEQ // TU
NCH = TU // 128

PK = {}
_off = 0


def _pk(name, n):
    global _off
    PK[name] = _off
    _off += n


_pk("ada_b", 192)
_pk("norm_g", 128)
_pk("a_conv_w", 48)
_pk("m_conv_w", 256)
_pk("m_conv_b", 64)
_pk("m_norm_g", 32)
_pk("m_dt_bias", 64)
_pk("m_a_log", 64)
_pk("m_d", 64)
NPK = _off


def pack_params(inp):
    pk = np.zeros((128, NPK), np.float32)

    def fm(v):
        v = np.asarray(v, np.float32)
        lead = v.shape[:-1]
        n = v.shape[-1] // 128
        v = v.reshape(lead + (n, 128))
        v = np.moveaxis(v, -1, 0)
        return v.reshape(128, -1)

    pk[:, PK["ada_b"]:PK["ada_b"] + 192] = fm(inp["ada_b"])
    pk[:, PK["norm_g"]:PK["norm_g"] + 128] = fm(inp["norm_g"])
    pk[:, PK["a_conv_w"]:PK["a_conv_w"] + 48] = fm(inp["a_conv_w"])
    pk[:, PK["m_conv_w"]:PK["m_conv_w"] + 256] = fm(inp["m_conv_w"])
    pk[:, PK["m_conv_b"]:PK["m_conv_b"] + 64] = fm(inp["m_conv_b"])
    pk[:, PK["m_norm_g"]:PK["m_norm_g"] + 32] = fm(inp["m_norm_g"])
    for nm in ("m_dt_bias", "m_a_log", "m_d"):
        pk[:, PK[nm]:PK[nm] + 64] = np.broadcast_to(
            np.asarray(inp[nm], np.float32).reshape(1, 64), (128, 64))
    return pk


import os as _os
SAME_ENGINE_RAW_ONLY = _os.environ.get("K_FULLSYNC", "0") != "1"


class Op:
    __slots__ = ("eng", "fn", "deps", "is_dma", "dsem", "dval", "inc", "count", "idx")


class Sched:
    ENGS = ("pe", "act", "dve", "pool", "sp")

    def __init__(self):
        self.ops = []
        self.lastw = {}
        self.readers = {}
        self.dma_vals = {}

    def add(self, eng, fn, reads=(), writes=(), dsem=None):
        op = Op()
        op.eng = eng
        op.fn = fn
        op.is_dma = dsem is not None
        op.dsem = dsem
        op.inc = False
        op.count = 0
        op.idx = len(self.ops)
        if op.is_dma:
            v = self.dma_vals.get(dsem, 0) + 16
            self.dma_vals[dsem] = v
            op.dval = v
        else:
            op.dval = 0
        deps = set()
        raw = set()
        for k in reads:
            w = self.lastw.get(k)
            if w is not None:
                deps.add(w)
                raw.add(w)
        for k in writes:
            w = self.lastw.get(k)
            if w is not None:
                deps.add(w)
            for r in self.readers.get(k, ()):
                deps.add(r)
        deps.discard(op)
        fdeps = []
        for d in deps:
            if d.is_dma and op.is_dma and d.dsem == op.dsem:
                continue
            if (not d.is_dma) and (not op.is_dma) and d.eng == op.eng:
                if d.eng == "pe" or (SAME_ENGINE_RAW_ONLY and d not in raw):
                    continue
            fdeps.append(d)
        op.deps = fdeps
        for k in reads:
            self.readers.setdefault(k, []).append(op)
        for k in writes:
            self.lastw[k] = op
            self.readers[k] = []
        self.ops.append(op)
        return op

    def emit(self, nc, sems, dsems):
        for op in self.ops:
            for d in op.deps:
                if not d.is_dma:
                    d.inc = True
        cnt = {e: 0 for e in self.ENGS}
        for op in self.ops:
            if (not op.is_dma) and op.inc:
                cnt[op.eng] += 1
                op.count = cnt[op.eng]
        by_eng = {e: [] for e in self.ENGS}
        for op in self.ops:
            by_eng[op.eng].append(op)

        def run(eng_name, eng):
            seen = {}
            for op in by_eng[eng_name]:
                need = {}
                for d in op.deps:
                    if d.is_dma:
                        key = ("d", d.dsem)
                        val = d.dval
                    else:
                        key = ("e", d.eng)
                        val = d.count
                    if need.get(key, 0) < val:
                        need[key] = val
                for key, val in need.items():
                    if seen.get(key, 0) < val:
                        sem = dsems[key[1]] if key[0] == "d" else sems[key[1]]
                        eng.wait_ge(sem, val)
                        seen[key] = val
                if op.fn is None:
                    continue
                ins = op.fn(eng)
                if op.is_dma:
                    ins.then_inc(dsems[op.dsem], 16)
                elif op.inc:
                    ins.then_inc(sems[op.eng], 1)

        with nc.Block() as block:
            @block.tensor
            def _(e):
                run("pe", e)

            @block.scalar
            def _(e):
                run("act", e)

            @block.vector
            def _(e):
                run("dve", e)

            @block.gpsimd
            def _(e):
                run("pool", e)

            @block.sync
            def _(e):
                run("sp", e)


def build_program(layers=(0, 1, 2, 3), nunits=NSEQ * UPS, debug_out=None):
    nc = bass.Bass("TRN2", target_bir_lowering=False)
    S = Sched()

    def din(name, shape):
        return nc.dram_tensor(name, list(shape), F32, kind="ExternalInput").ap()

    x_d = din("x_fm", (NSEQ, D, SEQ))
    cT_d = din("cT", (128, KD * NSEQ))
    pk_d = din("pk", (128, NPK))
    ada_w_d = din("ada_w", (D, 24 * D))
    a_w_in_d = din("a_w_in", (2, D, 3 * D))
    a_w_out_d = din("a_w_out", (2, D, D))
    m_w_in_d = din("m_w_in", (2, D, M_IN))
    m_w_out_d = din("m_w_out", (2, M_DI, D))
    f_w_in_d = din("f_w_in", (DEPTH, D, 2 * DFF))
    f_w_out_d = din("f_w_out", (DEPTH, DFF, D))
    out_d = nc.dram_tensor("out_fm", [NSEQ, D, SEQ], F32, kind="ExternalOutput").ap()

    import contextlib
    ctx = contextlib.ExitStack()
    with ctx:
        def sb(name, shape, dt=F32):
            return ctx.enter_context(nc.sbuf_tensor(name, list(shape), dt))

        def ps(name, shape, dt=F32):
            return ctx.enter_context(nc.psum_tensor(name, list(shape), dt))

        sems = {e: ctx.enter_context(nc.semaphore("sem_" + e)) for e in ("pe", "act", "dve", "pool")}
        dsems = {}

        def dsem(name):
            if name not in dsems:
                dsems[name] = ctx.enter_context(nc.semaphore("d_" + name))
            return name

        HC = 4
        xres = sb("xres", (128, KD, TU), F32)
        hy = sb("hy", (128, KD * TU), F32)
        ysb = hy[:].rearrange("p (m t) -> p m t", m=KD)
        hy_bf = hy[:].bitcast(BF16)
        hbuf = hy_bf[:, 0:KD * TU].rearrange("p (k t) -> p k t", k=KD)
        big = sb("big", (128, KF, TU), BF16)

        def hkeys(k, tt):
            return [("h", k, tt), ("ysb", k // 2, k % 2)]

        def ykeys(m, tt):
            ks = [("ysb", m, tt)]
            if 2 * m + tt < KD:
                ks += [("h", 2 * m + tt, 0), ("h", 2 * m + tt, 1)]
            return ks

        xbc_sets = [[hy_bf[:, KD * TU + q * TU:KD * TU + (q + 1) * TU] for q in range(4)], None]
        xbck_sets = [[("ysb", 4 + q // 2, q % 2) for q in range(4)], [("xbc1", q) for q in range(4)]]

        def hyt(ti):
            o = 6 * TU + ti * 256
            return hy[:, o:o + 256], ("ysb", 6 + ti // 4, (ti // 2) % 2)

        def bigcell(k, tt):
            return big[:, k, tt * 512:(tt + 1) * 512], ("big", k, tt)

        NSLOT = 3
        SLOTW = 8 * 768
        slabs = [sb("slab%d" % i, (128, SLOTW), BF16) for i in range(NSLOT)]
        pk = sb("pk_sb", (128, NPK), F32)
        cT = sb("cT_sb", (128, KD * NSEQ), F32)
        cs_bf = sb("cs_bf", (128, KD * NSEQ), BF16)
        modraw = sb("modraw", (128, 192 * NSEQ), F32)
        modp = sb("modp", (128, NSEQ * DEPTH * 2 * 3 * 8), F32)
        ones_bf = sb("ones_bf", (128, 128), BF16)
        mhalf = sb("mhalf", (128, 8), F32)
        sq = [sb("sq%d" % i, (128, 512), BF16) for i in range(3)]
        Fs = [sb("F%d" % i, (128, 512), F32) for i in range(5)]
        tmpf = csb = ub = var_sb = Fs
        rstd_sb = [sb("rstd%d" % i, (128, 512), F32) for i in range(2)]
        cvb = [sb("cvb%d" % i, (128, HC + TU), F32) for i in range(2)]
        halo_a = sb("halo_a", (128, 2 * 8 * 2), F32)
        Sst = sb("Sst", (128, 2 * 8 * 256), F32)
        halo_m = sb("halo_m", (128, 2 * 32 * 4), F32)
        wdt = sb("wdt", (128, 2 * KD * 32), BF16)
        ident_bf = sb("ident_bf", (128, 128), BF16)
        U_f32 = sb("U_f32", (128, 128), F32)
        U_bf = sb("U_bf", (128, 128), BF16)
        Mgt_bf = sb("Mgt_bf", (128, 128), BF16)
        ones_f32 = sb("ones_f32", (128, 128), F32)
        aexp = sb("aexp", (128, 64), F32)
        mcwh = sb("mcwh", (128, 256 + 64), F32)
        ssum = sb("ssum", (128, 8), F32)
        xbc1 = sb("xbc1", (128, 4 * TU), BF16)
        xbc_sets[1] = [xbc1[:, q * TU:(q + 1) * TU] for q in range(4)]
        hand2 = sb("hand2", (128, 1024), BF16)
        eps_col = sb("eps_col", (128, 1), F32)
        psb = [ps("ps%d" % i, (128, 512), F32) for i in range(8)]

        rot = {}

        def nxt(name, n):
            i = rot.get(name, 0)
            rot[name] = i + 1
            return i % n

        def ps_work():
            i = nxt("psw", 6)
            return psb[i], ("ps", i)

        def ps_ss():
            i = 6 + nxt("pss", 2)
            return psb[i], ("ps", i)

        def scr(lst, name):
            i = nxt(name, len(lst))
            return lst[i], (name, i)

        def slot():
            i = nxt("slot", NSLOT)
            return slabs[i], ("slab", i), dsem("slab%d" % i)

        def pcol(name, idx):
            o = PK[name] + idx
            return pk[:, o:o + 1]

        def modcol(b, i, sub, kind, m):
            o = ((((b * DEPTH + i) * 2 + sub) * 3 + kind) * 8) + m
            return modp[:, o:o + 1]

        S.add("sp", lambda e: e.dma_start(out=pk[:], in_=pk_d[:, :]), writes=["pk"], dsem=dsem("pk"))
        S.add("sp", lambda e: e.dma_start(out=cT[:], in_=cT_d[:, :]), writes=["cT"], dsem=dsem("cT"))
        S.add("dve", lambda e: e.memset(ones_bf[:], 1.0 / 1024.0), writes=["ones"])
        S.add("dve", lambda e: e.memset(mhalf[:], -0.5), writes=["mhalf"])
        S.add("dve", lambda e: e.memset(halo_a[:], 0.0), writes=["halo_a"])
        S.add("act", lambda e: e.activation(out=cs_bf[:], in_=cT[:], func=AF.Silu),
              reads=["cT"], writes=["cs"])
        ACH = 6
        pmod, pmod_key = psb[7], ("ps", 7)
        for sidx in range(192 // ACH):
            sl, slk, sld = slot()
            slv = sl[:, 0:KD * ACH * 128].rearrange("p (k c) -> p k c", k=KD)
            src = ada_w_d[:, sidx * ACH * 128:(sidx + 1) * ACH * 128].rearrange("(k p) c -> p k c", p=128)
            S.add("pool", lambda e, slv=slv, src=src: e.dma_start(out=slv, in_=src),
                  writes=[slk], dsem=sld)
            for jj in range(ACH):
                j = sidx * ACH + jj
                for k in range(KD):
                    S.add("pe", lambda e, j=j, jj=jj, k=k, slv=slv: e.matmul(
                        pmod[:, j * NSEQ:(j + 1) * NSEQ], lhsT=slv[:, k, jj * 128:(jj + 1) * 128],
                        rhs=cs_bf[:, k * NSEQ:(k + 1) * NSEQ], start=(k == 0), stop=(k == KD - 1)),
                        reads=[slk, "cs"], writes=[pmod_key])
        ab = pk[:, PK["ada_b"]:PK["ada_b"] + 192]
        S.add("dve", lambda e: e.tensor_tensor(
            out=modraw[:].rearrange("p (j b) -> p j b", b=NSEQ),
            in0=pmod[:, 0:192 * NSEQ].rearrange("p (j b) -> p j b", b=NSEQ),
            in1=ab.unsqueeze(2).to_broadcast([128, 192, NSEQ]), op=ALU.add),
            reads=[pmod_key, "pk"], writes=["modraw"])
        mr = modraw[:].rearrange("p (j b) -> p j b", b=NSEQ)
        for b in range(NSEQ):
            for i in range(DEPTH):
                for sub in range(2):
                    j0 = ((i * 2 + sub) * 3) * 8
                    o = (((b * DEPTH + i) * 2 + sub) * 3) * 8
                    ng_pre = pk[:, PK["norm_g"] + (i * 4 + 2 * sub) * 8:PK["norm_g"] + (i * 4 + 2 * sub) * 8 + 8]
                    ng_post = pk[:, PK["norm_g"] + (i * 4 + 2 * sub + 1) * 8:PK["norm_g"] + (i * 4 + 2 * sub + 1) * 8 + 8]
                    S.add("dve", lambda e, o=o, j0=j0, b=b, ng_pre=ng_pre: e.scalar_tensor_tensor(
                        out=modp[:, o:o + 8], in0=mr[:, j0 + 8:j0 + 16, b], scalar=1.0, in1=ng_pre,
                        op0=ALU.add, op1=ALU.mult), reads=["modraw", "pk"], writes=["modp"])
                    S.add("dve", lambda e, o=o, j0=j0, b=b: e.tensor_copy(
                        out=modp[:, o + 8:o + 16], in_=mr[:, j0:j0 + 8, b]),
                        reads=["modraw"], writes=["modp"])
                    S.add("dve", lambda e, o=o, j0=j0, b=b, ng_post=ng_post: e.tensor_tensor(
                        out=modp[:, o + 16:o + 24], in0=mr[:, j0 + 16:j0 + 24, b], in1=ng_post,
                        op=ALU.mult), reads=["modraw", "pk"], writes=["modp"])

        dif = Fs[4]
        S.add("pool", lambda e: e.iota(dif[:, 0:128], pattern=[[1, 128]], base=0, channel_multiplier=-1,
                                       allow_small_or_imprecise_dtypes=True), writes=["dif"])
        S.add("dve", lambda e: e.tensor_scalar(out=U_f32[:], in0=dif[:, 0:128], scalar1=0.0, scalar2=None,
                                               op0=ALU.is_ge), reads=["dif"], writes=["consts"])
        S.add("dve", lambda e: e.tensor_scalar(out=U_bf[:], in0=dif[:, 0:128], scalar1=0.0, scalar2=None,
                                               op0=ALU.is_ge), reads=["dif"], writes=["consts"])
        S.add("dve", lambda e: e.tensor_scalar(out=Mgt_bf[:], in0=dif[:, 0:128], scalar1=0.0, scalar2=None,
                                               op0=ALU.is_lt), reads=["dif"], writes=["consts"])
        S.add("dve", lambda e: e.tensor_scalar(out=ident_bf[:], in0=dif[:, 0:128], scalar1=0.0, scalar2=None,
                                               op0=ALU.is_equal), reads=["dif"], writes=["consts"])
        S.add("dve", lambda e: e.memset(ones_f32[:], 1.0), writes=["consts"])
        S.add("dve", lambda e: e.memset(eps_col[:], EPS), writes=["consts"])
        S.add("dve", lambda e: e.memset(Sst[:], 0.0), writes=[("S", jj, g) for jj in range(2) for g in range(8)])
        S.add("dve", lambda e: e.memset(halo_m[:], 0.0), writes=[("halo_m", 0), ("halo_m", 1)])
        S.add("act", lambda e: e.activation(out=aexp[:], in_=pk[:, PK["m_a_log"]:PK["m_a_log"] + 64], func=AF.Exp),
              reads=["pk"], writes=["consts2"])
        S.add("dve", lambda e: e.tensor_scalar(out=mcwh[:, 0:256], in0=pk[:, PK["m_conv_w"]:PK["m_conv_w"] + 256],
                                               scalar1=0.5, scalar2=None, op0=ALU.mult), reads=["pk"], writes=["consts2"])
        S.add("dve", lambda e: e.tensor_scalar(out=mcwh[:, 256:320], in0=pk[:, PK["m_conv_b"]:PK["m_conv_b"] + 64],
                                               scalar1=0.5, scalar2=None, op0=ALU.mult), reads=["pk"], writes=["consts2"])
        for jj in range(2):
            S.add("pool", lambda e, jj=jj: e.dma_start(
                out=wdt[:, jj * KD * 32:(jj + 1) * KD * 32].rearrange("p (k c) -> p k c", k=KD),
                in_=m_w_in_d[jj][:, 6144:6176].rearrange("(k p) c -> p k c", p=128)),
                writes=["wdt"], dsem=dsem("wdt"))

        def prenorm(b, i, sub):
            plist = []
            for tt in range(NT):
                tsl = slice(tt * 512, (tt + 1) * 512)
                pss, pssk = ps_ss()
                for k in range(KD):
                    s_, sk = scr(sq, "sq")
                    S.add("act", lambda e, s_=s_, k=k, tsl=tsl: e.activation(
                        out=s_[:], in_=xres[:, k, tsl], func=AF.Square),
                        reads=[("xres", k, tt)], writes=[sk])
                    S.add("pe", lambda e, s_=s_, k=k, pss=pss: e.matmul(
                        pss[:], lhsT=ones_bf[:], rhs=s_[:], start=(k == 0), stop=(k == KD - 1)),
                        reads=[sk, "ones"], writes=[pssk])
                plist.append((pss, pssk))
            rl = rstd_many(plist)
            for tt in range(NT):
                tsl = slice(tt * 512, (tt + 1) * 512)
                rs, rsk = rl[tt]
                for k in range(KD):
                    t_, tk = scr(Fs, "F")
                    S.add("dve", lambda e, t_=t_, k=k, tsl=tsl, rs=rs: e.tensor_tensor(
                        out=t_[:], in0=xres[:, k, tsl], in1=rs[:], op=ALU.mult),
                        reads=[("xres", k, tt), rsk], writes=[tk])
                    sc_ap = modcol(b, i, sub, 0, k)
                    bi_ap = modcol(b, i, sub, 1, k)
                    S.add("act", lambda e, t_=t_, k=k, tsl=tsl, sc_ap=sc_ap, bi_ap=bi_ap: e.activation(
                        out=hbuf[:, k, tsl], in_=t_[:], func=AF.Identity, scale=sc_ap, bias=bi_ap),
                        reads=[tk, "modp"], writes=hkeys(k, tt))

        def rstd_many(plist):
            vs = []
            for pss, pssk in plist:
                v_, vk = scr(Fs, "F")
                S.add("act", lambda e, v_=v_, pss=pss: e.activation(
                    out=v_[:], in_=pss[:], func=AF.Sqrt, bias=eps_col[:, 0:1], scale=1.0),
                    reads=[pssk, "consts"], writes=[vk])
                vs.append((v_, vk))
            outs = []
            for v_, vk in vs:
                rs, rsk = scr(rstd_sb, "rstd")
                S.add("dve", lambda e, v_=v_, rs=rs: e.reciprocal(out=rs[:], in_=v_[:]),
                      reads=[vk], writes=[rsk])
                outs.append((rs, rsk))
            return outs

        def out_proj_post(b, i, sub, w_d, nk):
            per = nk * 128
            pssl = [ps_ss() for tt in range(NT)]
            pend = []
            for m in range(KD):
                sl, slk, sld = slot()
                slv = sl[:, 0:per].rearrange("p (k c) -> p k c", k=nk)
                src = w_d[:, m * 128:(m + 1) * 128].rearrange("(k p) c -> p k c", p=128)
                S.add("pool", lambda e, slv=slv, src=src: e.dma_start(out=slv, in_=src),
                      writes=[slk], dsem=sld)
                for tt in range(NT):
                    tsl = slice(tt * 512, (tt + 1) * 512)
                    pss, pssk = pssl[tt]
                    pw, pwk = ps_work()
                    for k in range(nk):
                        S.add("pe", lambda e, pw=pw, slv=slv, k=k, tsl=tsl: e.matmul(
                            pw[:], lhsT=slv[:, k, :], rhs=big[:, k, tsl],
                            start=(k == 0), stop=(k == nk - 1)),
                            reads=[slk, ("big", k, tt)], writes=[pwk])
                    for f in pend:
                        f()
                    pend = []
                    S.add("act", lambda e, pw=pw, m=m, tsl=tsl: e.activation(
                        out=ysb[:, m, tsl], in_=pw[:], func=AF.Copy),
                        reads=[pwk], writes=ykeys(m, tt))
                    s_, sk = scr(sq, "sq")
                    S.add("act", lambda e, pw=pw, s_=s_: e.activation(
                        out=s_[:], in_=pw[:], func=AF.Square),
                        reads=[pwk], writes=[sk])

                    def ssmm(s_=s_, sk=sk, m=m, pss=pss, pssk=pssk):
                        S.add("pe", lambda e: e.matmul(
                            pss[:], lhsT=ones_bf[:], rhs=s_[:], start=(m == 0), stop=(m == KD - 1)),
                            reads=[sk, "ones"], writes=[pssk])
                    pend.append(ssmm)
            for f in pend:
                f()
            rl = rstd_many(pssl)
            for tt in range(NT):
                tsl = slice(tt * 512, (tt + 1) * 512)
                rs, rsk = rl[tt]
                for m in range(KD):
                    t_, tk = scr(Fs, "F")
                    gg_ap = modcol(b, i, sub, 2, m)
                    S.add("dve", lambda e, t_=t_, m=m, rs=rs, tsl=tsl, gg_ap=gg_ap: e.scalar_tensor_tensor(
                        out=t_[:], in0=ysb[:, m, tsl], scalar=gg_ap, in1=rs[:],
                        op0=ALU.mult, op1=ALU.mult),
                        reads=[("ysb", m, tt), rsk, "modp"], writes=[tk])
                    S.add("dve", lambda e, t_=t_, m=m, tsl=tsl: e.tensor_tensor(
                        out=xres[:, m, tsl], in0=xres[:, m, tsl], in1=t_[:], op=ALU.add),
                        reads=[tk, ("xres", m, tt)], writes=[("xres", m, tt)])

        def conv_mixer(b, i, first_unit):
            j = i // 2
            w_in = a_w_in_d[j]
            if first_unit:
                S.add("dve", lambda e: e.memset(halo_a[:, j * 16:(j + 1) * 16], 0.0),
                      reads=[], writes=[("halo_a", j)])
            for m in range(KD):
                sl, slk, sld = slot()
                slv = sl[:, 0:KD * 3 * 128].rearrange("p (k t c) -> p k t c", k=KD, t=3)
                for t3 in range(3):
                    src = w_in[:, (t3 * 8 + m) * 128:(t3 * 8 + m + 1) * 128].rearrange("(k p) c -> p k c", p=128)
                    S.add("pool", lambda e, slv=slv, t3=t3, src=src: e.dma_start(
                        out=slv[:, :, t3, :], in_=src), writes=[slk], dsem=sld)
                cv, cvk = scr(cvb, "cvb")
                ho = (j * 8 + m) * 2
                S.add("dve", lambda e, cv=cv, ho=ho: e.tensor_copy(out=cv[:, HC - 2:HC], in_=halo_a[:, ho:ho + 2]),
                      reads=[("halo_a", j)], writes=[(cvk, "h")])
                w0, w1, w2 = [pcol("a_conv_w", (j * 3 + kk) * 8 + m) for kk in range(3)]
                for tt in range(NT):
                    tsl = slice(tt * 512, (tt + 1) * 512)
                    pp = []
                    for t3 in range(3):
                        pw, pwk = ps_work()
                        for k in range(KD):
                            S.add("pe", lambda e, pw=pw, slv=slv, t3=t3, k=k, tsl=tsl: e.matmul(
                                pw[:], lhsT=slv[:, k, t3, :], rhs=hbuf[:, k, tsl],
                                start=(k == 0), stop=(k == KD - 1)),
                                reads=[slk, ("h", k, tt)], writes=[pwk])
                        pp.append((pw, pwk))
                    (pb, pbk), (pc, pck), (pv, pvk) = pp
                    c_, ck = scr(Fs, "F")
                    S.add("act", lambda e, c_=c_, pc=pc: e.activation(out=c_[:], in_=pc[:], func=AF.Copy),
                          reads=[pck], writes=[ck])
                    S.add("dve", lambda e, cv=cv, pv=pv, c_=c_, tt=tt: e.tensor_tensor(
                        out=cv[:, HC + tt * 512:HC + (tt + 1) * 512], in0=pv[:], in1=c_[:], op=ALU.mult),
                        reads=[pvk, ck], writes=[(cvk, tt)])
                    u_, uk = scr(Fs, "F")
                    rk = [(cvk, tt), (cvk, "h") if tt == 0 else (cvk, tt - 1)]
                    S.add("dve", lambda e, u_=u_, cv=cv, tt=tt, w0=w0: e.tensor_scalar(
                        out=u_[:], in0=cv[:, HC - 2 + tt * 512:HC - 2 + tt * 512 + 512], scalar1=w0, scalar2=None, op0=ALU.mult),
                        reads=rk + ["pk"], writes=[uk])
                    S.add("dve", lambda e, u_=u_, cv=cv, tt=tt, w1=w1: e.scalar_tensor_tensor(
                        out=u_[:], in0=cv[:, HC - 1 + tt * 512:HC - 1 + tt * 512 + 512], scalar=w1, in1=u_[:],
                        op0=ALU.mult, op1=ALU.add), reads=rk + [uk, "pk"], writes=[uk])
                    S.add("dve", lambda e, u_=u_, cv=cv, tt=tt, w2=w2: e.scalar_tensor_tensor(
                        out=u_[:], in0=cv[:, HC + tt * 512:HC + tt * 512 + 512], scalar=w2, in1=u_[:],
                        op0=ALU.mult, op1=ALU.add), reads=rk + [uk, "pk"], writes=[uk])
                    S.add("dve", lambda e, u_=u_, pb=pb, tsl=tsl, m=m: e.tensor_tensor(
                        out=big[:, m, tsl], in0=pb[:], in1=u_[:], op=ALU.mult),
                        reads=[pbk, uk], writes=[("big", m, tt)])
                S.add("dve", lambda e, cv=cv, ho=ho: e.tensor_copy(out=halo_a[:, ho:ho + 2], in_=cv[:, HC + TU - 2:HC + TU]),
                      reads=[(cvk, NT - 1)], writes=[("halo_a", j)])
            out_proj_post(b, i, 0, a_w_out_d[j], KD)

        def ffn(b, i):
            w_in = f_w_in_d[i]
            for m in range(KF):
                sl, slk, sld = slot()
                slv = sl[:, 0:KD * 2 * 128].rearrange("p (k t c) -> p k t c", k=KD, t=2)
                for t2 in range(2):
                    src = w_in[:, (t2 * KF + m) * 128:(t2 * KF + m + 1) * 128].rearrange("(k p) c -> p k c", p=128)
                    S.add("pool", lambda e, slv=slv, t2=t2, src=src: e.dma_start(
                        out=slv[:, :, t2, :], in_=src), writes=[slk], dsem=sld)
                for tt in range(NT):
                    tsl = slice(tt * 512, (tt + 1) * 512)
                    pp = []
                    for t2 in range(2):
                        pw, pwk = ps_work()
                        for k in range(KD):
                            S.add("pe", lambda e, pw=pw, slv=slv, t2=t2, k=k, tsl=tsl: e.matmul(
                                pw[:], lhsT=slv[:, k, t2, :], rhs=hbuf[:, k, tsl],
                                start=(k == 0), stop=(k == KD - 1)),
                                reads=[slk, ("h", k, tt)], writes=[pwk])
                        pp.append((pw, pwk))
                    (pg, pgk), (pu, puk) = pp
                    c_, ck = scr(Fs, "F")
                    S.add("act", lambda e, c_=c_, pg=pg: e.activation(out=c_[:], in_=pg[:], func=AF.Silu),
                          reads=[pgk], writes=[ck])
                    S.add("dve", lambda e, pu=pu, c_=c_, m=m, tsl=tsl: e.tensor_tensor(
                        out=big[:, m, tsl], in0=pu[:], in1=c_[:], op=ALU.mult),
                        reads=[puk, ck], writes=[("big", m, tt)])
            out_proj_post(b, i, 1, f_w_out_d[i], KF)

        def interleave(items):
            act_ = [list(it) for it in items if it[0] is not None]
            while act_:
                for it in list(act_):
                    for _ in range(it[1]):
                        try:
                            next(it[0])
                        except StopIteration:
                            act_.remove(it)
                            break

        def ssd_mixer(b, i, first_unit):
            j = i // 2
            w_in = m_w_in_d[j]
            if first_unit:
                S.add("dve", lambda e: e.memset(Sst[:, j * 2048:(j + 1) * 2048], 0.0),
                      writes=[("S", j, g) for g in range(8)])
                S.add("dve", lambda e: e.memset(halo_m[:, j * 128:(j + 1) * 128], 0.0), writes=[("halo_m", j)])
            dt_tm, dt_k = hyt(0)
            dta, dta_k = hyt(1)
            ecs, ecs_k = hyt(2)
            cd, cd_k = hyt(3)
            dtdte, dtdte_k = hyt(4)
            dIs = [hyt(6), hyt(7)]
            v3 = lambda ap: ap.rearrange("p (c h) -> p c h", h=32)
            pdt, pdtk = ps_work()
            for c in range(NCH):
                for k in range(KD):
                    S.add("pe", lambda e, c=c, k=k: e.matmul(
                        pdt[:, c * 32:(c + 1) * 32], lhsT=hbuf[:, k, c * 128:(c + 1) * 128],
                        rhs=wdt[:, (j * KD + k) * 32:(j * KD + k + 1) * 32], start=(k == 0), stop=(k == KD - 1)),
                        reads=[("h", k, c // 4), "wdt"], writes=[pdtk])
            f0, f0k = scr(Fs, "F")
            dtb = pk[:, PK["m_dt_bias"] + j * 32:PK["m_dt_bias"] + (j + 1) * 32]
            S.add("dve", lambda e: e.tensor_tensor(
                out=v3(f0[:, 0:256]), in0=v3(pdt[:, 0:256]), in1=dtb.unsqueeze(1).to_broadcast([128, NCH, 32]),
                op=ALU.add), reads=[pdtk, "pk"], writes=[f0k])
            f1, f1k = scr(Fs, "F")
            S.add("act", lambda e: e.activation(out=f1[:, 0:256], in_=f0[:, 0:256], func=AF.Exp),
                  reads=[f0k], writes=[f1k])
            S.add("act", lambda e: e.activation(out=dt_tm, in_=f1[:, 0:256], func=AF.Ln, bias=1.0, scale=1.0),
                  reads=[f1k], writes=[dt_k])
            ae = aexp[:, j * 32:(j + 1) * 32]
            S.add("dve", lambda e: e.scalar_tensor_tensor(
                out=v3(dta), in0=v3(dt_tm), scalar=-1.0, in1=ae.unsqueeze(1).to_broadcast([128, NCH, 32]),
                op0=ALU.mult, op1=ALU.mult), reads=[dt_k, "consts2"], writes=[dta_k])
            pcs, pcsk = ps_work()
            ptot, ptotk = ps_work()
            for c in range(NCH):
                S.add("pe", lambda e, c=c: e.matmul(pcs[:, c * 32:(c + 1) * 32], lhsT=U_f32[:],
                                                    rhs=dta[:, c * 32:(c + 1) * 32], start=True, stop=True),
                      reads=[dta_k, "consts"], writes=[pcsk])
                S.add("pe", lambda e, c=c: e.matmul(ptot[:, c * 32:(c + 1) * 32], lhsT=ones_f32[:],
                                                    rhs=dta[:, c * 32:(c + 1) * 32], start=True, stop=True),
                      reads=[dta_k, "consts"], writes=[ptotk])
            S.add("act", lambda e: e.activation(out=ecs, in_=pcs[:, 0:256], func=AF.Exp), reads=[pcsk], writes=[ecs_k])
            S.add("act", lambda e: e.activation(out=cd, in_=ptot[:, 0:256], func=AF.Exp), reads=[ptotk], writes=[cd_k])
            f2, f2k = scr(Fs, "F")
            S.add("act", lambda e: e.activation(out=f2[:, 0:256], in_=pcs[:, 0:256], func=AF.Copy),
                  reads=[pcsk], writes=[f2k])
            f3, f3k = scr(Fs, "F")
            S.add("dve", lambda e: e.tensor_tensor(out=f3[:, 0:256], in0=ptot[:, 0:256], in1=f2[:, 0:256],
                                                   op=ALU.subtract), reads=[ptotk, f2k], writes=[f3k])
            f4, f4k = scr(Fs, "F")
            S.add("act", lambda e: e.activation(out=f4[:, 0:256], in_=f3[:, 0:256], func=AF.Exp),
                  reads=[f3k], writes=[f4k])
            S.add("dve", lambda e: e.tensor_tensor(out=dtdte, in0=dt_tm, in1=f4[:, 0:256], op=ALU.mult),
                  reads=[dt_k, f4k], writes=[dtdte_k])

            cA, cAk = bigcell(16, 0)
            cB0, cB0k = bigcell(16, 1)
            dtaU, dtaUk = bigcell(17, 0)
            DT, DTk = bigcell(17, 1)
            MT, MTk = bigcell(18, 0)
            cF, t1k = bigcell(18, 1)
            cG, ytk = bigcell(19, 0)
            cH, thzk = bigcell(19, 1)
            cI, zs20k = bigcell(20, 0)
            cJ, yzk = bigcell(20, 1)
            cK, junkk = bigcell(21, 0)
            cL, cLk = bigcell(21, 1)
            xs_tm = cA[:, 0:256]
            xdt = cA[:, 256:512]
            cBs = [(cB0, cB0k), (hand2[:, 0:512], "hand2a")]
            zs2s = [(cI.bitcast(F32), zs20k), (hand2[:, 512:1024].bitcast(F32), "hand2b")]
            t1 = cF.bitcast(F32)
            yt = cG.bitcast(F32)
            thz = cH.bitcast(F32)
            yzs = [(cJ.bitcast(F32), yzk), (sq[0][:].bitcast(F32), ("sq", 0))]
            junk = cK.bitcast(F32)
            yn = cL[:, 0:256]
            S_bf = cL[:, 256:512]
            h4 = lambda ap: ap.rearrange("p (r x) -> p r x", r=4)

            slab_of = {}

            def load_slab(g):
                sl, slk, sld = slot()
                slv = sl[:, :].rearrange("p (k c) -> p k c", k=KD)
                for (c0, w, s0) in ((0, 256, g * 256), (256, 256, 2048 + g * 256),
                                    (512, 128, 4096 + g * 128), (640, 128, 5120 + g * 128)):
                    src = w_in[:, s0:s0 + w].rearrange("(k p) c -> p k c", p=128)
                    S.add("pool", lambda e, c0=c0, w=w, src=src, slv=slv: e.dma_start(
                        out=slv[:, :, c0:c0 + w], in_=src), writes=[slk], dsem=sld)
                slab_of[g] = (slv, slk)

            def bank(ix):
                return psb[ix], ("ps", ix)

            def inproj_gen(g):
                slv, slk = slab_of[g]
                xbc_g = xbc_sets[g % 2]
                xbck = xbck_sets[g % 2]
                dI, dIk = dIs[g % 2]
                dIb = dI.bitcast(BF16)
                for r in range(4):
                    dcol = pcol("m_d", j * 32 + 4 * g + r)
                    S.add("dve", lambda e, r=r, dcol=dcol, dIb=dIb: e.tensor_scalar(
                        out=dIb[:, r * 128:(r + 1) * 128], in0=ident_bf[:], scalar1=dcol, scalar2=None,
                        op0=ALU.mult), reads=["consts", "pk"], writes=[dIk])
                yield
                for q in range(4):
                    mq = (2 * g, 2 * g + 1, 16 + g, 24 + g)[q]
                    cv, cvk = scr(cvb, "cvb")
                    cvh = cv[:].bitcast(BF16)
                    ho = (j * 32 + mq) * 4
                    S.add("dve", lambda e, cvh=cvh, ho=ho: e.tensor_copy(out=cvh[:, HC - 4:HC], in_=halo_m[:, ho:ho + 4]),
                          reads=[("halo_m", j)], writes=[(cvk, "h")])
                    wh = [mcwh[:, (j * 4 + kk) * 32 + mq:(j * 4 + kk) * 32 + mq + 1] for kk in range(4)]
                    bh = mcwh[:, 256 + j * 32 + mq:256 + j * 32 + mq + 1]
                    dset = nxt("dgset", 2)
                    dgt = rstd_sb[dset][:].bitcast(BF16)
                    dgk = ("rstd", dset)
                    for kk in range(4):
                        S.add("dve", lambda e, dgt=dgt, kk=kk, wh=wh: e.tensor_scalar(
                            out=dgt[:, kk * 128:(kk + 1) * 128], in0=ident_bf[:], scalar1=wh[kk], scalar2=None,
                            op0=ALU.mult), reads=["consts", "consts2"], writes=[dgk])
                    yield
                    for tt in range(NT):
                        tsl = slice(tt * 512, (tt + 1) * 512)
                        pw, pwk = bank(6)
                        for k in range(KD):
                            S.add("pe", lambda e, pw=pw, slv=slv, q=q, k=k, tsl=tsl: e.matmul(
                                pw[:], lhsT=slv[:, k, 256 + q * 128:256 + (q + 1) * 128], rhs=hbuf[:, k, tsl],
                                start=(k == 0), stop=(k == KD - 1)),
                                reads=[slk, ("h", k, tt)], writes=[pwk])
                            if k % 2 == 1:
                                yield
                        S.add("act", lambda e, cvh=cvh, pw=pw, tt=tt: e.activation(
                            out=cvh[:, HC + tt * 512:HC + (tt + 1) * 512], in_=pw[:], func=AF.Copy),
                            reads=[pwk], writes=[(cvk, tt)])
                        yield
                        rk = [(cvk, tt), (cvk, "h") if tt == 0 else (cvk, tt - 1)]
                        pc, pck = bank(7)
                        for kk in range(4):
                            S.add("pe", lambda e, pc=pc, dgt=dgt, kk=kk, cvh=cvh, tt=tt: e.matmul(
                                pc[:], lhsT=dgt[:, kk * 128:(kk + 1) * 128],
                                rhs=cvh[:, HC - 3 + kk + tt * 512:HC - 3 + kk + tt * 512 + 512],
                                start=(kk == 0), stop=(kk == 3)), reads=rk + [dgk], writes=[pck])
                        yield
                        hx, hxk = scr(Fs, "F")
                        S.add("dve", lambda e, hx=hx, pc=pc, bh=bh: e.tensor_scalar(
                            out=hx[:], in0=pc[:], scalar1=bh, scalar2=None, op0=ALU.add),
                            reads=[pck, "consts2"], writes=[hxk])
                        yield
                        th, thk = scr(Fs, "F")
                        S.add("act", lambda e, th=th, hx=hx: e.activation(out=th[:], in_=hx[:], func=AF.Tanh),
                              reads=[hxk], writes=[thk])
                        yield
                        S.add("dve", lambda e, th=th, hx=hx, q=q, tsl=tsl, xbc_g=xbc_g: e.scalar_tensor_tensor(
                            out=xbc_g[q][:, tsl], in0=th[:], scalar=1.0, in1=hx[:], op0=ALU.add, op1=ALU.mult),
                            reads=[thk, hxk], writes=[xbck[q]])
                        yield
                    S.add("dve", lambda e, cvh=cvh, ho=ho: e.tensor_copy(
                        out=halo_m[:, ho:ho + 4], in_=cvh[:, HC + TU - 4:HC + TU]),
                        reads=[(cvk, NT - 1)], writes=[("halo_m", j)])
                    yield

            def scanA_gen(g, c, hs):
                slv, slk = slab_of[g]
                xbc_g = xbc_sets[g % 2]
                xbck = xbck_sets[g % 2]
                dI, dIk = dIs[g % 2]
                dIb = dI.bitcast(BF16)
                csl = slice(c * 128, (c + 1) * 128)
                hsl = slice(c * 32 + 4 * g, c * 32 + 4 * g + 4)
                cB, cBk = cBs[c % 2]
                zs2, zs2k = zs2s[c % 2]
                xdte = cB[:, 0:256]
                B_tm = cB[:, 256:384]
                CBm = cB[:, 384:512]
                ptr, ptrk = bank(0)
                ptb = ptr[:].bitcast(BF16)
                for q in range(3):
                    S.add("pe", lambda e, q=q, ptb=ptb, csl=csl: e.transpose(
                        out=ptb[:, q * 128:(q + 1) * 128], in_=xbc_g[q][:, csl], identity=ident_bf[:]),
                        reads=[xbck[q], "consts"], writes=[ptrk])
                pcb, pcbk = ptr[:, 256:512], ptrk
                S.add("pe", lambda e, pcb=pcb, csl=csl: e.matmul(
                    pcb[:, 0:128], lhsT=xbc_g[2][:, csl], rhs=xbc_g[3][:, csl], start=True, stop=True),
                    reads=[xbck[2], xbck[3]], writes=[pcbk])
                yield
                S.add("pool", lambda e, hsl=hsl: e.tensor_tensor(
                    out=h4(dtaU), in0=U_bf[:].unsqueeze(1).to_broadcast([128, 4, 128]),
                    in1=dta[:, hsl].unsqueeze(2).to_broadcast([128, 4, 128]), op=ALU.mult),
                    reads=[dta_k, "consts"], writes=[dtaUk])
                yield
                pseg, psegk = bank(1)
                for r in range(4):
                    S.add("pe", lambda e, pseg=pseg, r=r: e.matmul(
                        pseg[:, r * 128:(r + 1) * 128], lhsT=Mgt_bf[:], rhs=dtaU[:, r * 128:(r + 1) * 128],
                        start=True, stop=True), reads=[dtaUk, "consts"], writes=[psegk])
                yield
                S.add("act", lambda e, ptb=ptb: e.activation(out=xs_tm, in_=ptb[:, 0:256], func=AF.Copy),
                      reads=[ptrk], writes=[cAk])
                yield
                S.add("act", lambda e, ptb=ptb: e.activation(out=B_tm, in_=ptb[:, 256:384], func=AF.Copy),
                      reads=[ptrk], writes=[cBk])
                yield
                S.add("dve", lambda e, ptb=ptb, hsl=hsl: e.tensor_tensor(
                    out=h4(xdt), in0=h4(ptb[:, 0:256]), in1=dt_tm[:, hsl].unsqueeze(2).to_broadcast([128, 4, 64]),
                    op=ALU.mult), reads=[ptrk, dt_k], writes=[cAk])
                yield
                S.add("dve", lambda e, ptb=ptb, hsl=hsl: e.tensor_tensor(
                    out=h4(xdte), in0=h4(ptb[:, 0:256]), in1=dtdte[:, hsl].unsqueeze(2).to_broadcast([128, 4, 64]),
                    op=ALU.mult), reads=[ptrk, dtdte_k], writes=[cBk])
                yield
                S.add("dve", lambda e, pcb=pcb: e.tensor_tensor(out=CBm, in0=pcb[:, 0:128], in1=U_f32[:], op=ALU.mult),
                      reads=[pcbk, "consts"], writes=[cBk])
                yield
                S.add("act", lambda e, pseg=pseg: e.activation(out=DT, in_=pseg[:], func=AF.Exp),
                      reads=[psegk], writes=[DTk])
                yield
                pz, pzk = bank(2)
                for k in range(KD):
                    S.add("pe", lambda e, pz=pz, k=k, csl=csl, slv=slv: e.matmul(
                        pz[:, 0:256], lhsT=hbuf[:, k, csl], rhs=slv[:, k, 0:256], start=(k == 0), stop=(k == KD - 1)),
                        reads=[slk, ("h", k, c // 4)], writes=[pzk])
                    if k % 4 == 3:
                        yield
                S.add("dve", lambda e: e.tensor_tensor(
                    out=h4(MT), in0=h4(DT), in1=CBm.unsqueeze(1).to_broadcast([128, 4, 128]), op=ALU.mult),
                    reads=[DTk, cBk], writes=[MTk])
                yield
                S.add("act", lambda e, pz=pz: e.activation(out=thz, in_=pz[:, 0:256], func=AF.Tanh, scale=0.5),
                      reads=[pzk], writes=[thzk])
                yield
                py, pyk = bank(3 + c % 2)
                for r in range(4):
                    S.add("pe", lambda e, py=py, r=r: e.matmul(
                        py[:, r * 64:(r + 1) * 64], lhsT=MT[:, r * 128:(r + 1) * 128], rhs=xdt[:, r * 64:(r + 1) * 64],
                        start=True, stop=False), reads=[MTk, cAk], writes=[pyk])
                    S.add("pe", lambda e, py=py, r=r, dIb=dIb: e.matmul(
                        py[:, r * 64:(r + 1) * 64], lhsT=dIb[:, r * 128:(r + 1) * 128], rhs=xs_tm[:, r * 64:(r + 1) * 64],
                        start=False, stop=True), reads=[dIk, cAk], writes=[pyk])
                yield
                S.add("dve", lambda e, pz=pz, zs2=zs2: e.scalar_tensor_tensor(
                    out=zs2, in0=thz, scalar=1.0, in1=pz[:, 0:256], op0=ALU.add, op1=ALU.mult),
                    reads=[thzk, pzk], writes=[zs2k])
                yield
                hs[c] = (py, pyk)

            def scanB_gen(g, c, hs):
                xbc_g = xbc_sets[g % 2]
                xbck = xbck_sets[g % 2]
                csl = slice(c * 128, (c + 1) * 128)
                hsl = slice(c * 32 + 4 * g, c * 32 + 4 * g + 4)
                cB, cBk = cBs[c % 2]
                zs2, zs2k = zs2s[c % 2]
                xdte = cB[:, 0:256]
                B_tm = cB[:, 256:384]
                py, pyk = hs[c]
                Sg = Sst[:, (j * 8 + g) * 256:(j * 8 + g + 1) * 256]
                Sk = ("S", j, g)
                S.add("act", lambda e, Sg=Sg: e.activation(out=S_bf, in_=Sg, func=AF.Copy),
                      reads=[Sk], writes=[cLk])
                yield
                S.add("pe", lambda e, py=py, csl=csl: e.matmul(
                    py[:, 256:512], lhsT=xbc_g[3][:, csl], rhs=S_bf, start=True, stop=True),
                    reads=[xbck[3], cLk], writes=[pyk])
                pst, pstk = bank(5)
                S.add("pe", lambda e, pst=pst: e.matmul(pst[:, 0:256], lhsT=B_tm, rhs=xdte, start=True, stop=True),
                      reads=[cBk], writes=[pstk])
                yield
                S.add("dve", lambda e, py=py, hsl=hsl: e.tensor_tensor(
                    out=h4(t1), in0=h4(py[:, 256:512]), in1=ecs[:, hsl].unsqueeze(2).to_broadcast([128, 4, 64]),
                    op=ALU.mult), reads=[pyk, ecs_k], writes=[t1k])
                yield
                S.add("dve", lambda e, py=py: e.tensor_tensor(out=yt, in0=py[:, 0:256], in1=t1, op=ALU.add),
                      reads=[pyk, t1k], writes=[ytk])
                yield
                yz, yzk = yzs[c % 2]
                yn = sq[1 + c % 2][:, 0:256]
                ynk = ("sq", 1 + c % 2)
                S.add("dve", lambda e, zs2=zs2, yz=yz: e.tensor_tensor(out=yz, in0=yt, in1=zs2, op=ALU.mult),
                      reads=[ytk, zs2k], writes=[yzk])
                yield
                S.add("dve", lambda e, Sg=Sg, hsl=hsl: e.tensor_tensor(
                    out=h4(Sg), in0=h4(Sg), in1=cd[:, hsl].unsqueeze(2).to_broadcast([128, 4, 64]), op=ALU.mult),
                    reads=[Sk, cd_k], writes=[Sk])
                yield
                S.add("dve", lambda e, Sg=Sg, pst=pst: e.tensor_tensor(out=Sg, in0=pst[:, 0:256], in1=Sg, op=ALU.add),
                      reads=[Sk, pstk], writes=[Sk])
                yield
                si = nxt("ssum", 4)
                ss_ap = ssum[:, si:si + 1]
                vv_ap = ssum[:, 4 + si:5 + si]
                ssk = ("ssum", si)
                S.add("act", lambda e, ss_ap=ss_ap, yz=yz: e.activation(out=junk, in_=yz, func=AF.Square, scale=0.5,
                                                                 accum_out=ss_ap),
                      reads=[yzk], writes=[junkk, ssk])
                yield
                S.add("dve", lambda e, ss_ap=ss_ap: e.tensor_scalar(
                    out=ss_ap, in0=ss_ap, scalar1=1.0 / 64.0, scalar2=4.0 * EPS, op0=ALU.mult, op1=ALU.add),
                    reads=[ssk], writes=[ssk])
                yield
                S.add("pool", lambda e, ss_ap=ss_ap, vv_ap=vv_ap: e.tensor_tensor(
                    out=vv_ap, in0=ss_ap, in1=mhalf[:, 0:1], op=ALU.pow),
                    reads=[ssk, "mhalf"], writes=[("ssv", si)])
                yield
                S.add("act", lambda e, vv_ap=vv_ap, yz=yz, yn=yn: e.activation(out=yn, in_=yz, func=AF.Copy, scale=vv_ap),
                      reads=[yzk, ("ssv", si)], writes=[ynk])
                yield
                pt2, pt2k = bank(5)
                pt2b = pt2[:].bitcast(BF16)[:, 512:1024]
                for q in range(2):
                    S.add("pe", lambda e, q=q, pt2b=pt2b, yn=yn: e.transpose(
                        out=pt2b[:, q * 128:(q + 1) * 128], in_=yn[:, q * 128:(q + 1) * 128], identity=ident_bf[:]),
                        reads=[ynk, "consts"], writes=[pt2k])
                yield
                for q in range(2):
                    gcol = pcol("m_norm_g", j * 16 + 2 * g + q)
                    S.add("act", lambda e, q=q, pt2b=pt2b, gcol=gcol, csl=csl, g=g: e.activation(
                        out=big[:, 2 * g + q, csl], in_=pt2b[:, q * 128:(q + 1) * 128], func=AF.Copy, scale=gcol),
                        reads=[pt2k, "pk"], writes=[("big", 2 * g + q, c // 4)])
                    yield

            def scan_gen(g):
                hs = {}
                for _ in scanA_gen(g, 0, hs):
                    yield
                for c in range(NCH):
                    gb = scanB_gen(g, c, hs)
                    ga = scanA_gen(g, c + 1, hs) if c + 1 < NCH else None
                    done_a = ga is None
                    done_b = False
                    while not (done_a and done_b):
                        if not done_b:
                            try:
                                next(gb)
                            except StopIteration:
                                done_b = True
                            yield
                        if not done_a:
                            try:
                                next(ga)
                            except StopIteration:
                                done_a = True
                            yield

            load_slab(0)
            load_slab(1)
            for _ in inproj_gen(0):
                pass
            for g in range(8):
                if g + 2 < 8:
                    load_slab(g + 2)
                interleave([[scan_gen(g), 3], [inproj_gen(g + 1) if g + 1 < 8 else None, 1]])
            out_proj_post(b, i, 0, m_w_out_d[j], 16)


        for u in range(nunits):
            b = u // UPS
            uh = u % UPS
            tok0 = uh * TU
            for k in range(KD):
                S.add("sp", lambda e, k=k, b=b, tok0=tok0: e.dma_start(
                    out=xres[:, k, :], in_=x_d[b, k * 128:(k + 1) * 128, tok0:tok0 + TU]),
                    writes=[("xres", k, tt) for tt in range(NT)], dsem=dsem("xin%d" % k))
            for i in layers:
                prenorm(b, i, 0)
                if i % 2 == 0:
                    conv_mixer(b, i, uh == 0)
                else:
                    ssd_mixer(b, i, uh == 0)
                prenorm(b, i, 1)
                ffn(b, i)
            for k in range(KD):
                S.add("sp", lambda e, k=k, b=b, tok0=tok0: e.dma_start(
                    out=out_d[b, k * 128:(k + 1) * 128, tok0:tok0 + TU], in_=xres[:, k, :]),
                    reads=[("xres", k, tt) for tt in range(NT)], writes=[("outd", k)],
                    dsem=dsem("xout%d" % k))
        S.add("sp", None, reads=[("outd", k) for k in range(KD)])

        S.emit(nc, sems, dsems)
    return nc


_CACHE = {}


def make_in_maps(inp):
    x = np.asarray(inp["x"], np.float32)
    c = np.asarray(inp["c"], np.float32)
    pk = pack_params(inp)
    shared = {
        "pk": pk,
        "ada_w": np.ascontiguousarray(inp["ada_w"], np.float32),
        "a_w_in": np.ascontiguousarray(inp["a_w_in"], np.float32),
        "a_w_out": np.ascontiguousarray(inp["a_w_out"], np.float32),
        "m_w_in": np.ascontiguousarray(inp["m_w_in"], np.float32),
        "m_w_out": np.ascontiguousarray(inp["m_w_out"], np.float32),
        "f_w_in": np.ascontiguousarray(inp["f_w_in"], np.float32),
        "f_w_out": np.ascontiguousarray(inp["f_w_out"], np.float32),
    }
    maps = []
    for r in range(NCORES):
        xs = x[r * NSEQ:(r + 1) * NSEQ]
        x_fm = np.ascontiguousarray(np.transpose(xs, (0, 2, 1)))
        cc = c[r * NSEQ:(r + 1) * NSEQ]
        cT = np.ascontiguousarray(
            np.transpose(cc.reshape(NSEQ, KD, 128), (2, 1, 0)).reshape(128, KD * NSEQ))
        m = dict(shared)
        m["x_fm"] = x_fm
        m["cT"] = cT
        maps.append(m)
    return maps


def kernel(**inputs):
    if "nc" not in _CACHE:
        _CACHE["nc"] = build_program()
    nc = _CACHE["nc"]
    maps = make_in_maps(inputs)
    res = run_bass_kernel_spmd(nc, maps, core_ids=list(range(NCORES)))
    outs = [np.transpose(np.asarray(r["out_fm"]), (0, 2, 1)) for r in res.results]
    return np.ascontiguousarray(np.concatenate(outs, axis=0).astype(np.float32))
```

```python
import numpy as np
import concourse.bass as bass
import concourse.mybir as mybir
from concourse.bass_utils import run_bass_kernel_spmd

F32 = mybir.dt.float32
BF16 = mybir.dt.bfloat16
AF = mybir.ActivationFunctionType
ALU = mybir.AluOpType

NCORES = 8
D = 1024
KD = 8
SEQ = 2048
DEPTH = 4
DFF = 2816
KF = 22
M_DI = 2048
M_IN = 6176
EPS = 1e-6
NSEQ = 2
NT = 2
TU = 512 * NT
UPS = SEQ // TU
NCH = TU // 128

PK = {}
_off = 0


def _pk(name, n):
    global _off
    PK[name] = _off
    _off += n


_pk("ada_b", 192)
_pk("norm_g", 128)
_pk("a_conv_w", 48)
_pk("m_conv_w", 256)
_pk("m_conv_b", 64)
_pk("m_norm_g", 32)
_pk("m_dt_bias", 64)
_pk("m_a_log", 64)
_pk("m_d", 64)
NPK = _off


def pack_params(inp):
    pk = np.zeros((128, NPK), np.float32)

    def fm(v):
        v = np.asarray(v, np.float32)
        lead = v.shape[:-1]
        n = v.shape[-1] // 128
        v = v.reshape(lead + (n, 128))
        v = np.moveaxis(v, -1, 0)
        return v.reshape(128, -1)

    pk[:, PK["ada_b"]:PK["ada_b"] + 192] = fm(inp["ada_b"])
    pk[:, PK["norm_g"]:PK["norm_g"] + 128] = fm(inp["norm_g"])
    pk[:, PK["a_conv_w"]:PK["a_conv_w"] + 48] = fm(inp["a_conv_w"])
    pk[:, PK["m_conv_w"]:PK["m_conv_w"] + 256] = fm(inp["m_conv_w"])
    pk[:, PK["m_conv_b"]:PK["m_conv_b"] + 64] = fm(inp["m_conv_b"])
    pk[:, PK["m_norm_g"]:PK["m_norm_g"] + 32] = fm(inp["m_norm_g"])
    for nm in ("m_dt_bias", "m_a_log", "m_d"):
        pk[:, PK[nm]:PK[nm] + 64] = np.broadcast_to(
            np.asarray(inp[nm], np.float32).reshape(1, 64), (128, 64))
    return pk


import os as _os
SAME_ENGINE_RAW_ONLY = _os.environ.get("K_FULLSYNC", "0") != "1"


class Op:
    __slots__ = ("eng", "fn", "deps", "is_dma", "dsem", "dval", "inc", "count", "idx")


class Sched:
    ENGS = ("pe", "act", "dve", "pool", "sp")

    def __init__(self):
        self.ops = []
        self.lastw = {}
        self.readers = {}
        self.dma_vals = {}

    def add(self, eng, fn, reads=(), writes=(), dsem=None):
        op = Op()
        op.eng = eng
        op.fn = fn
        op.is_dma = dsem is not None
        op.dsem = dsem
        op.inc = False
        op.count = 0
        op.idx = len(self.ops)
        if op.is_dma:
            v = self.dma_vals.get(dsem, 0) + 16
            self.dma_vals[dsem] = v
            op.dval = v
        else:
            op.dval = 0
        deps = set()
        raw = set()
        for k in reads:
            w = self.lastw.get(k)
            if w is not None:
                deps.add(w)
                raw.add(w)
        for k in writes:
            w = self.lastw.get(k)
            if w is not None:
                deps.add(w)
            for r in self.readers.get(k, ()):
                deps.add(r)
        deps.discard(op)
        fdeps = []
        for d in deps:
            if d.is_dma and op.is_dma and d.dsem == op.dsem:
                continue
            if (not d.is_dma) and (not op.is_dma) and d.eng == op.eng:
                if d.eng == "pe" or (SAME_ENGINE_RAW_ONLY and d not in raw):
                    continue
            fdeps.append(d)
        op.deps = fdeps
        for k in reads:
            self.readers.setdefault(k, []).append(op)
        for k in writes:
            self.lastw[k] = op
            self.readers[k] = []
        self.ops.append(op)
        return op

    def emit(self, nc, sems, dsems):
        for op in self.ops:
            for d in op.deps:
                if not d.is_dma:
                    d.inc = True
        cnt = {e: 0 for e in self.ENGS}
        for op in self.ops:
            if (not op.is_dma) and op.inc:
                cnt[op.eng] += 1
                op.count = cnt[op.eng]
        by_eng = {e: [] for e in self.ENGS}
        for op in self.ops:
            by_eng[op.eng].append(op)

        def run(eng_name, eng):
            seen = {}
            for op in by_eng[eng_name]:
                need = {}
                for d in op.deps:
                    if d.is_dma:
                        key = ("d", d.dsem)
                        val = d.dval
                    else:
                        key = ("e", d.eng)
                        val = d.count
                    if need.get(key, 0) < val:
                        need[key] = val
                for key, val in need.items():
                    if seen.get(key, 0) < val:
                        sem = dsems[key[1]] if key[0] == "d" else sems[key[1]]
                        eng.wait_ge(sem, val)
                        seen[key] = val
                if op.fn is None:
                    continue
                ins = op.fn(eng)
                if op.is_dma:
                    ins.then_inc(dsems[op.dsem], 16)
                elif op.inc:
                    ins.then_inc(sems[op.eng], 1)

        with nc.Block() as block:
            @block.tensor
            def _(e):
                run("pe", e)

            @block.scalar
            def _(e):
                run("act", e)

            @block.vector
            def _(e):
                run("dve", e)

            @block.gpsimd
            def _(e):
                run("pool", e)

            @block.sync
            def _(e):
                run("sp", e)


def build_program(layers=(0, 1, 2, 3), nunits=NSEQ * UPS, debug_out=None):
    nc = bass.Bass("TRN2", target_bir_lowering=False)
    S = Sched()

    def din(name, shape):
        return nc.dram_tensor(name, list(shape), F32, kind="ExternalInput").ap()

    x_d = din("x_fm", (NSEQ, D, SEQ))
    cT_d = din("cT", (128, KD * NSEQ))
    pk_d = din("pk", (128, NPK))
    ada_w_d = din("ada_w", (D, 24 * D))
    a_w_in_d = din("a_w_in", (2, D, 3 * D))
    a_w_out_d = din("a_w_out", (2, D, D))
    m_w_in_d = din("m_w_in", (2, D, M_IN))
    m_w_out_d = din("m_w_out", (2, M_DI, D))
    f_w_in_d = din("f_w_in", (DEPTH, D, 2 * DFF))
    f_w_out_d = din("f_w_out", (DEPTH, DFF, D))
    out_d = nc.dram_tensor("out_fm", [NSEQ, D, SEQ], F32, kind="ExternalOutput").ap()

    import contextlib
    ctx = contextlib.ExitStack()
    with ctx:
        def sb(name, shape, dt=F32):
            return ctx.enter_context(nc.sbuf_tensor(name, list(shape), dt))

        def ps(name, shape, dt=F32):
            return ctx.enter_context(nc.psum_tensor(name, list(shape), dt))

        sems = {e: ctx.enter_context(nc.semaphore("sem_" + e)) for e in ("pe", "act", "dve", "pool")}
        dsems = {}

        def dsem(name):
            if name not in dsems:
                dsems[name] = ctx.enter_context(nc.semaphore("d_" + name))
            return name

        HC = 4
        xres = sb("xres", (128, KD, TU), F32)
        hy = sb("hy", (128, KD * TU), F32)
        ysb = hy[:].rearrange("p (m t) -> p m t", m=KD)
        hy_bf = hy[:].bitcast(BF16)
        hbuf = hy_bf[:, 0:KD * TU].rearrange("p (k t) -> p k t", k=KD)
        big = sb("big", (128, KF, TU), BF16)

        def hkeys(k, tt):
            return [("h", k, tt), ("ysb", k // 2, k % 2)]

        def ykeys(m, tt):
            ks = [("ysb", m, tt)]
            if 2 * m + tt < KD:
                ks += [("h", 2 * m + tt, 0), ("h", 2 * m + tt, 1)]
            return ks

        xbc_sets = [[hy_bf[:, KD * TU + q * TU:KD * TU + (q + 1) * TU] for q in range(4)], None]
        xbck_sets = [[("ysb", 4 + q // 2, q % 2) for q in range(4)], [("xbc1", q) for q in range(4)]]

        def hyt(ti):
            o = 6 * TU + ti * 256
            return hy[:, o:o + 256], ("ysb", 6 + ti // 4, (ti // 2) % 2)

        def bigcell(k, tt):
            return big[:, k, tt * 512:(tt + 1) * 512], ("big", k, tt)

        NSLOT = 3
        SLOTW = 8 * 768
        slabs = [sb("slab%d" % i, (128, SLOTW), BF16) for i in range(NSLOT)]
        pk = sb("pk_sb", (128, NPK), F32)
        cT = sb("cT_sb", (128, KD * NSEQ), F32)
        cs_bf = sb("cs_bf", (128, KD * NSEQ), BF16)
        modraw = sb("modraw", (128, 192 * NSEQ), F32)
        modp = sb("modp", (128, NSEQ * DEPTH * 2 * 3 * 8), F32)
        ones_bf = sb("ones_bf", (128, 128), BF16)
        mhalf = sb("mhalf", (128, 8), F32)
        sq = [sb("sq%d" % i, (128, 512), BF16) for i in range(3)]
        Fs = [sb("F%d" % i, (128, 512), F32) for i in range(5)]
        tmpf = csb = ub = var_sb = Fs
        rstd_sb = [sb("rstd%d" % i, (128, 512), F32) for i in range(2)]
        cvb = [sb("cvb%d" % i, (128, HC + TU), F32) for i in range(2)]
        halo_a = sb("halo_a", (128, 2 * 8 * 2), F32)
        Sst = sb("Sst", (128, 2 * 8 * 256), F32)
        halo_m = sb("halo_m", (128, 2 * 32 * 4), F32)
        wdt = sb("wdt", (128, 2 * KD * 32), BF16)
        ident_bf = sb("ident_bf", (128, 128), BF16)
        U_f32 = sb("U_f32", (128, 128), F32)
        U_bf = sb("U_bf", (128, 128), BF16)
        Mgt_bf = sb("Mgt_bf", (128, 128), BF16)
        ones_f32 = sb("ones_f32", (128, 128), F32)
        aexp = sb("aexp", (128, 64), F32)
        mcwh = sb("mcwh", (128, 256 + 64), F32)
        ssum = sb("ssum", (128, 8), F32)
        xbc1 = sb("xbc1", (128, 4 * TU), BF16)
        xbc_sets[1] = [xbc1[:, q * TU:(q + 1) * TU] for q in range(4)]
        hand2 = sb("hand2", (128, 1024), BF16)
        eps_col = sb("eps_col", (128, 1), F32)
        eps4_col = sb("eps4_col", (128, 1), F32)
        psb = [ps("ps%d" % i, (128, 512), F32) for i in range(8)]

        rot = {}

        def nxt(name, n):
            i = rot.get(name, 0)
            rot[name] = i + 1
            return i % n

        def ps_work():
            i = nxt("psw", 6)
            return psb[i], ("ps", i)

        def ps_ss():
            i = 6 + nxt("pss", 2)
            return psb[i], ("ps", i)

        def scr(lst, name):
            i = nxt(name, len(lst))
            return lst[i], (name, i)

        def slot():
            i = nxt("slot", NSLOT)
            return slabs[i], ("slab", i), dsem("slab%d" % i)

        def pcol(name, idx):
            o = PK[name] + idx
            return pk[:, o:o + 1]

        def modcol(b, i, sub, kind, m):
            o = ((((b * DEPTH + i) * 2 + sub) * 3 + kind) * 8) + m
            return modp[:, o:o + 1]

        S.add("sp", lambda e: e.dma_start(out=pk[:], in_=pk_d[:, :]), writes=["pk"], dsem=dsem("pk"))
        S.add("sp", lambda e: e.dma_start(out=cT[:], in_=cT_d[:, :]), writes=["cT"], dsem=dsem("cT"))
        S.add("dve", lambda e: e.memset(ones_bf[:], 1.0 / 1024.0), writes=["ones"])
        S.add("dve", lambda e: e.memset(mhalf[:], -0.5), writes=["mhalf"])
        S.add("dve", lambda e: e.memset(halo_a[:], 0.0), writes=["halo_a"])
        S.add("act", lambda e: e.activation(out=cs_bf[:], in_=cT[:], func=AF.Silu),
              reads=["cT"], writes=["cs"])
        ACH = 6
        pmod, pmod_key = psb[7], ("ps", 7)
        for sidx in range(192 // ACH):
            sl, slk, sld = slot()
            slv = sl[:, 0:KD * ACH * 128].rearrange("p (k c) -> p k c", k=KD)
            src = ada_w_d[:, sidx * ACH * 128:(sidx + 1) * ACH * 128].rearrange("(k p) c -> p k c", p=128)
            S.add("pool", lambda e, slv=slv, src=src: e.dma_start(out=slv, in_=src),
                  writes=[slk], dsem=sld)
            for jj in range(ACH):
                j = sidx * ACH + jj
                for k in range(KD):
                    S.add("pe", lambda e, j=j, jj=jj, k=k, slv=slv: e.matmul(
                        pmod[:, j * NSEQ:(j + 1) * NSEQ], lhsT=slv[:, k, jj * 128:(jj + 1) * 128],
                        rhs=cs_bf[:, k * NSEQ:(k + 1) * NSEQ], start=(k == 0), stop=(k == KD - 1)),
                        reads=[slk, "cs"], writes=[pmod_key])
        ab = pk[:, PK["ada_b"]:PK["ada_b"] + 192]
        S.add("dve", lambda e: e.tensor_tensor(
            out=modraw[:].rearrange("p (j b) -> p j b", b=NSEQ),
            in0=pmod[:, 0:192 * NSEQ].rearrange("p (j b) -> p j b", b=NSEQ),
            in1=ab.unsqueeze(2).to_broadcast([128, 192, NSEQ]), op=ALU.add),
            reads=[pmod_key, "pk"], writes=["modraw"])
        mr = modraw[:].rearrange("p (j b) -> p j b", b=NSEQ)
        for b in range(NSEQ):
            for i in range(DEPTH):
                for sub in range(2):
                    j0 = ((i * 2 + sub) * 3) * 8
                    o = (((b * DEPTH + i) * 2 + sub) * 3) * 8
                    ng_pre = pk[:, PK["norm_g"] + (i * 4 + 2 * sub) * 8:PK["norm_g"] + (i * 4 + 2 * sub) * 8 + 8]
                    ng_post = pk[:, PK["norm_g"] + (i * 4 + 2 * sub + 1) * 8:PK["norm_g"] + (i * 4 + 2 * sub + 1) * 8 + 8]
                    S.add("dve", lambda e, o=o, j0=j0, b=b, ng_pre=ng_pre: e.scalar_tensor_tensor(
                        out=modp[:, o:o + 8], in0=mr[:, j0 + 8:j0 + 16, b], scalar=1.0, in1=ng_pre,
                        op0=ALU.add, op1=ALU.mult), reads=["modraw", "pk"], writes=["modp"])
                    S.add("dve", lambda e, o=o, j0=j0, b=b: e.tensor_copy(
                        out=modp[:, o + 8:o + 16], in_=mr[:, j0:j0 + 8, b]),
                        reads=["modraw"], writes=["modp"])
                    S.add("dve", lambda e, o=o, j0=j0, b=b, ng_post=ng_post: e.tensor_tensor(
                        out=modp[:, o + 16:o + 24], in0=mr[:, j0 + 16:j0 + 24, b], in1=ng_post,
                        op=ALU.mult), reads=["modraw", "pk"], writes=["modp"])

        dif = Fs[4]
        S.add("pool", lambda e: e.iota(dif[:, 0:128], pattern=[[1, 128]], base=0, channel_multiplier=-1,
                                       allow_small_or_imprecise_dtypes=True), writes=["dif"])
        S.add("dve", lambda e: e.tensor_scalar(out=U_f32[:], in0=dif[:, 0:128], scalar1=0.0, scalar2=None,
                                               op0=ALU.is_ge), reads=["dif"], writes=["consts"])
        S.add("dve", lambda e: e.tensor_scalar(out=U_bf[:], in0=dif[:, 0:128], scalar1=0.0, scalar2=None,
                                               op0=ALU.is_ge), reads=["dif"], writes=["consts"])
        S.add("dve", lambda e: e.tensor_scalar(out=Mgt_bf[:], in0=dif[:, 0:128], scalar1=0.0, scalar2=None,
                                               op0=ALU.is_lt), reads=["dif"], writes=["consts"])
        S.add("dve", lambda e: e.tensor_scalar(out=ident_bf[:], in0=dif[:, 0:128], scalar1=0.0, scalar2=None,
                                               op0=ALU.is_equal), reads=["dif"], writes=["consts"])
        S.add("dve", lambda e: e.memset(ones_f32[:], 1.0), writes=["consts"])
        S.add("dve", lambda e: e.memset(eps_col[:], EPS), writes=["consts"])
        S.add("dve", lambda e: e.memset(eps4_col[:], 4.0 * EPS), writes=["consts"])
        S.add("dve", lambda e: e.memset(Sst[:], 0.0), writes=[("S", jj, g) for jj in range(2) for g in range(8)])
        S.add("dve", lambda e: e.memset(halo_m[:], 0.0), writes=[("halo_m", 0), ("halo_m", 1)])
        S.add("act", lambda e: e.activation(out=aexp[:], in_=pk[:, PK["m_a_log"]:PK["m_a_log"] + 64], func=AF.Exp),
              reads=["pk"], writes=["consts2"])
        S.add("dve", lambda e: e.tensor_scalar(out=mcwh[:, 0:256], in0=pk[:, PK["m_conv_w"]:PK["m_conv_w"] + 256],
                                               scalar1=0.5, scalar2=None, op0=ALU.mult), reads=["pk"], writes=["consts2"])
        S.add("dve", lambda e: e.tensor_scalar(out=mcwh[:, 256:320], in0=pk[:, PK["m_conv_b"]:PK["m_conv_b"] + 64],
                                               scalar1=0.5, scalar2=None, op0=ALU.mult), reads=["pk"], writes=["consts2"])
        for jj in range(2):
            S.add("pool", lambda e, jj=jj: e.dma_start(
                out=wdt[:, jj * KD * 32:(jj + 1) * KD * 32].rearrange("p (k c) -> p k c", k=KD),
                in_=m_w_in_d[jj][:, 6144:6176].rearrange("(k p) c -> p k c", p=128)),
                writes=["wdt"], dsem=dsem("wdt"))

        def prenorm(b, i, sub):
            plist = []
            for tt in range(NT):
                tsl = slice(tt * 512, (tt + 1) * 512)
                pss, pssk = ps_ss()
                for k in range(KD):
                    s_, sk = scr(sq, "sq")
                    S.add("act", lambda e, s_=s_, k=k, tsl=tsl: e.activation(
                        out=s_[:], in_=xres[:, k, tsl], func=AF.Square),
                        reads=[("xres", k, tt)], writes=[sk])
                    S.add("pe", lambda e, s_=s_, k=k, pss=pss: e.matmul(
                        pss[:], lhsT=ones_bf[:], rhs=s_[:], start=(k == 0), stop=(k == KD - 1)),
                        reads=[sk, "ones"], writes=[pssk])
                plist.append((pss, pssk))
            rl = rstd_many(plist)
            for tt in range(NT):
                tsl = slice(tt * 512, (tt + 1) * 512)
                rs, rsk = rl[tt]
                for k in range(KD):
                    t_, tk = scr(Fs, "F")
                    S.add("dve", lambda e, t_=t_, k=k, tsl=tsl, rs=rs: e.tensor_tensor(
                        out=t_[:], in0=xres[:, k, tsl], in1=rs[:], op=ALU.mult),
                        reads=[("xres", k, tt), rsk], writes=[tk])
                    sc_ap = modcol(b, i, sub, 0, k)
                    bi_ap = modcol(b, i, sub, 1, k)
                    S.add("act", lambda e, t_=t_, k=k, tsl=tsl, sc_ap=sc_ap, bi_ap=bi_ap: e.activation(
                        out=hbuf[:, k, tsl], in_=t_[:], func=AF.Identity, scale=sc_ap, bias=bi_ap),
                        reads=[tk, "modp"], writes=hkeys(k, tt))

        def rstd_many(plist):
            vs = []
            for pss, pssk in plist:
                v_, vk = scr(Fs, "F")
                S.add("act", lambda e, v_=v_, pss=pss: e.activation(
                    out=v_[:], in_=pss[:], func=AF.Sqrt, bias=eps_col[:, 0:1], scale=1.0),
                    reads=[pssk, "consts"], writes=[vk])
                vs.append((v_, vk))
            outs = []
            for v_, vk in vs:
                rs, rsk = scr(rstd_sb, "rstd")
                S.add("dve", lambda e, v_=v_, rs=rs: e.reciprocal(out=rs[:], in_=v_[:]),
                      reads=[vk], writes=[rsk])
                outs.append((rs, rsk))
            return outs

        def out_proj_post(b, i, sub, w_d, nk):
            per = nk * 128
            pssl = [ps_ss() for tt in range(NT)]
            pend = []
            for m in range(KD):
                sl, slk, sld = slot()
                slv = sl[:, 0:per].rearrange("p (k c) -> p k c", k=nk)
                src = w_d[:, m * 128:(m + 1) * 128].rearrange("(k p) c -> p k c", p=128)
                S.add("pool", lambda e, slv=slv, src=src: e.dma_start(out=slv, in_=src),
                      writes=[slk], dsem=sld)
                for tt in range(NT):
                    tsl = slice(tt * 512, (tt + 1) * 512)
                    pss, pssk = pssl[tt]
                    pw, pwk = ps_work()
                    for k in range(nk):
                        S.add("pe", lambda e, pw=pw, slv=slv, k=k, tsl=tsl: e.matmul(
                            pw[:], lhsT=slv[:, k, :], rhs=big[:, k, tsl],
                            start=(k == 0), stop=(k == nk - 1)),
                            reads=[slk, ("big", k, tt)], writes=[pwk])
                    for f in pend:
                        f()
                    pend = []
                    S.add("act", lambda e, pw=pw, m=m, tsl=tsl: e.activation(
                        out=ysb[:, m, tsl], in_=pw[:], func=AF.Copy),
                        reads=[pwk], writes=ykeys(m, tt))
                    s_, sk = scr(sq, "sq")
                    S.add("act", lambda e, pw=pw, s_=s_: e.activation(
                        out=s_[:], in_=pw[:], func=AF.Square),
                        reads=[pwk], writes=[sk])

                    def ssmm(s_=s_, sk=sk, m=m, pss=pss, pssk=pssk):
                        S.add("pe", lambda e: e.matmul(
                            pss[:], lhsT=ones_bf[:], rhs=s_[:], start=(m == 0), stop=(m == KD - 1)),
                            reads=[sk, "ones"], writes=[pssk])
                    pend.append(ssmm)
            for f in pend:
                f()
            rl = rstd_many(pssl)
            for tt in range(NT):
                tsl = slice(tt * 512, (tt + 1) * 512)
                rs, rsk = rl[tt]
                for m in range(KD):
                    t_, tk = scr(Fs, "F")
                    gg_ap = modcol(b, i, sub, 2, m)
                    S.add("dve", lambda e, t_=t_, m=m, rs=rs, tsl=tsl, gg_ap=gg_ap: e.scalar_tensor_tensor(
                        out=t_[:], in0=ysb[:, m, tsl], scalar=gg_ap, in1=rs[:],
                        op0=ALU.mult, op1=ALU.mult),
                        reads=[("ysb", m, tt), rsk, "modp"], writes=[tk])
                    S.add("dve", lambda e, t_=t_, m=m, tsl=tsl: e.tensor_tensor(
                        out=xres[:, m, tsl], in0=xres[:, m, tsl], in1=t_[:], op=ALU.add),
                        reads=[tk, ("xres", m, tt)], writes=[("xres", m, tt)])

        def conv_mixer(b, i, first_unit):
            j = i // 2
            w_in = a_w_in_d[j]
            if first_unit:
                S.add("dve", lambda e: e.memset(halo_a[:, j * 16:(j + 1) * 16], 0.0),
                      reads=[], writes=[("halo_a", j)])
            for m in range(KD):
                sl, slk, sld = slot()
                slv = sl[:, 0:KD * 3 * 128].rearrange("p (k t c) -> p k t c", k=KD, t=3)
                for t3 in range(3):
                    src = w_in[:, (t3 * 8 + m) * 128:(t3 * 8 + m + 1) * 128].rearrange("(k p) c -> p k c", p=128)
                    S.add("pool", lambda e, slv=slv, t3=t3, src=src: e.dma_start(
                        out=slv[:, :, t3, :], in_=src), writes=[slk], dsem=sld)
                cv, cvk = scr(cvb, "cvb")
                ho = (j * 8 + m) * 2
                S.add("dve", lambda e, cv=cv, ho=ho: e.tensor_copy(out=cv[:, HC - 2:HC], in_=halo_a[:, ho:ho + 2]),
                      reads=[("halo_a", j)], writes=[(cvk, "h")])
                w0, w1, w2 = [pcol("a_conv_w", (j * 3 + kk) * 8 + m) for kk in range(3)]
                for tt in range(NT):
                    tsl = slice(tt * 512, (tt + 1) * 512)
                    pp = []
                    for t3 in range(3):
                        pw, pwk = ps_work()
                        for k in range(KD):
                            S.add("pe", lambda e, pw=pw, slv=slv, t3=t3, k=k, tsl=tsl: e.matmul(
                                pw[:], lhsT=slv[:, k, t3, :], rhs=hbuf[:, k, tsl],
                                start=(k == 0), stop=(k == KD - 1)),
                                reads=[slk, ("h", k, tt)], writes=[pwk])
                        pp.append((pw, pwk))
                    (pb, pbk), (pc, pck), (pv, pvk) = pp
                    c_, ck = scr(Fs, "F")
                    S.add("act", lambda e, c_=c_, pc=pc: e.activation(out=c_[:], in_=pc[:], func=AF.Copy),
                          reads=[pck], writes=[ck])
                    S.add("dve", lambda e, cv=cv, pv=pv, c_=c_, tt=tt: e.tensor_tensor(
                        out=cv[:, HC + tt * 512:HC + (tt + 1) * 512], in0=pv[:], in1=c_[:], op=ALU.mult),
                        reads=[pvk, ck], writes=[(cvk, tt)])
                    u_, uk = scr(Fs, "F")
                    rk = [(cvk, tt), (cvk, "h") if tt == 0 else (cvk, tt - 1)]
                    S.add("dve", lambda e, u_=u_, cv=cv, tt=tt, w0=w0: e.tensor_scalar(
                        out=u_[:], in0=cv[:, HC - 2 + tt * 512:HC - 2 + tt * 512 + 512], scalar1=w0, scalar2=None, op0=ALU.mult),
                        reads=rk + ["pk"], writes=[uk])
                    S.add("dve", lambda e, u_=u_, cv=cv, tt=tt, w1=w1: e.scalar_tensor_tensor(
                        out=u_[:], in0=cv[:, HC - 1 + tt * 512:HC - 1 + tt * 512 + 512], scalar=w1, in1=u_[:],
                        op0=ALU.mult, op1=ALU.add), reads=rk + [uk, "pk"], writes=[uk])
                    S.add("dve", lambda e, u_=u_, cv=cv, tt=tt, w2=w2: e.scalar_tensor_tensor(
                        out=u_[:], in0=cv[:, HC + tt * 512:HC + tt * 512 + 512], scalar=w2, in1=u_[:],
                        op0=ALU.mult, op1=ALU.add), reads=rk + [uk, "pk"], writes=[uk])
                    S.add("dve", lambda e, u_=u_, pb=pb, tsl=tsl, m=m: e.tensor_tensor(
                        out=big[:, m, tsl], in0=pb[:], in1=u_[:], op=ALU.mult),
                        reads=[pbk, uk], writes=[("big", m, tt)])
                S.add("dve", lambda e, cv=cv, ho=ho: e.tensor_copy(out=halo_a[:, ho:ho + 2], in_=cv[:, HC + TU - 2:HC + TU]),
                      reads=[(cvk, NT - 1)], writes=[("halo_a", j)])
            out_proj_post(b, i, 0, a_w_out_d[j], KD)

        def ffn(b, i):
            w_in = f_w_in_d[i]
            for m in range(KF):
                sl, slk, sld = slot()
                slv = sl[:, 0:KD * 2 * 128].rearrange("p (k t c) -> p k t c", k=KD, t=2)
                for t2 in range(2):
                    src = w_in[:, (t2 * KF + m) * 128:(t2 * KF + m + 1) * 128].rearrange("(k p) c -> p k c", p=128)
                    S.add("pool", lambda e, slv=slv, t2=t2, src=src: e.dma_start(
                        out=slv[:, :, t2, :], in_=src), writes=[slk], dsem=sld)
                for tt in range(NT):
                    tsl = slice(tt * 512, (tt + 1) * 512)
                    pp = []
                    for t2 in range(2):
                        pw, pwk = ps_work()
                        for k in range(KD):
                            S.add("pe", lambda e, pw=pw, slv=slv, t2=t2, k=k, tsl=tsl: e.matmul(
                                pw[:], lhsT=slv[:, k, t2, :], rhs=hbuf[:, k, tsl],
                                start=(k == 0), stop=(k == KD - 1)),
                                reads=[slk, ("h", k, tt)], writes=[pwk])
                        pp.append((pw, pwk))
                    (pg, pgk), (pu, puk) = pp
                    c_, ck = scr(Fs, "F")
                    S.add("act", lambda e, c_=c_, pg=pg: e.activation(out=c_[:], in_=pg[:], func=AF.Silu),
                          reads=[pgk], writes=[ck])
                    S.add("dve", lambda e, pu=pu, c_=c_, m=m, tsl=tsl: e.tensor_tensor(
                        out=big[:, m, tsl], in0=pu[:], in1=c_[:], op=ALU.mult),
                        reads=[puk, ck], writes=[("big", m, tt)])
            out_proj_post(b, i, 1, f_w_out_d[i], KF)

        def interleave(items):
            act_ = [list(it) for it in items if it[0] is not None]
            while act_:
                for it in list(act_):
                    for _ in range(it[1]):
                        try:
                            next(it[0])
                        except StopIteration:
                            act_.remove(it)
                            break

        def ssd_mixer(b, i, first_unit):
            j = i // 2
            w_in = m_w_in_d[j]
            if first_unit:
                S.add("dve", lambda e: e.memset(Sst[:, j * 2048:(j + 1) * 2048], 0.0),
                      writes=[("S", j, g) for g in range(8)])
                S.add("dve", lambda e: e.memset(halo_m[:, j * 128:(j + 1) * 128], 0.0), writes=[("halo_m", j)])
            dt_tm, dt_k = hyt(0)
            dta, dta_k = hyt(1)
            ecs, ecs_k = hyt(2)
            cd, cd_k = hyt(3)
            dtdte, dtdte_k = hyt(4)
            dIs = [hyt(6), hyt(7)]
            v3 = lambda ap: ap.rearrange("p (c h) -> p c h", h=32)
            pdt, pdtk = ps_work()
            for c in range(NCH):
                for k in range(KD):
                    S.add("pe", lambda e, c=c, k=k: e.matmul(
                        pdt[:, c * 32:(c + 1) * 32], lhsT=hbuf[:, k, c * 128:(c + 1) * 128],
                        rhs=wdt[:, (j * KD + k) * 32:(j * KD + k + 1) * 32], start=(k == 0), stop=(k == KD - 1)),
                        reads=[("h", k, c // 4), "wdt"], writes=[pdtk])
            f0, f0k = scr(Fs, "F")
            dtb = pk[:, PK["m_dt_bias"] + j * 32:PK["m_dt_bias"] + (j + 1) * 32]
            S.add("dve", lambda e: e.tensor_tensor(
                out=v3(f0[:, 0:256]), in0=v3(pdt[:, 0:256]), in1=dtb.unsqueeze(1).to_broadcast([128, NCH, 32]),
                op=ALU.add), reads=[pdtk, "pk"], writes=[f0k])
            f1, f1k = scr(Fs, "F")
            S.add("act", lambda e: e.activation(out=f1[:, 0:256], in_=f0[:, 0:256], func=AF.Exp),
                  reads=[f0k], writes=[f1k])
            S.add("act", lambda e: e.activation(out=dt_tm, in_=f1[:, 0:256], func=AF.Ln, bias=1.0, scale=1.0),
                  reads=[f1k], writes=[dt_k])
            ae = aexp[:, j * 32:(j + 1) * 32]
            S.add("dve", lambda e: e.scalar_tensor_tensor(
                out=v3(dta), in0=v3(dt_tm), scalar=-1.0, in1=ae.unsqueeze(1).to_broadcast([128, NCH, 32]),
                op0=ALU.mult, op1=ALU.mult), reads=[dt_k, "consts2"], writes=[dta_k])
            pcs, pcsk = ps_work()
            ptot, ptotk = ps_work()
            for c in range(NCH):
                S.add("pe", lambda e, c=c: e.matmul(pcs[:, c * 32:(c + 1) * 32], lhsT=U_f32[:],
                                                    rhs=dta[:, c * 32:(c + 1) * 32], start=True, stop=True),
                      reads=[dta_k, "consts"], writes=[pcsk])
                S.add("pe", lambda e, c=c: e.matmul(ptot[:, c * 32:(c + 1) * 32], lhsT=ones_f32[:],
                                                    rhs=dta[:, c * 32:(c + 1) * 32], start=True, stop=True),
                      reads=[dta_k, "consts"], writes=[ptotk])
            S.add("act", lambda e: e.activation(out=ecs, in_=pcs[:, 0:256], func=AF.Exp), reads=[pcsk], writes=[ecs_k])
            S.add("act", lambda e: e.activation(out=cd, in_=ptot[:, 0:256], func=AF.Exp), reads=[ptotk], writes=[cd_k])
            f2, f2k = scr(Fs, "F")
            S.add("act", lambda e: e.activation(out=f2[:, 0:256], in_=pcs[:, 0:256], func=AF.Copy),
                  reads=[pcsk], writes=[f2k])
            f3, f3k = scr(Fs, "F")
            S.add("dve", lambda e: e.tensor_tensor(out=f3[:, 0:256], in0=ptot[:, 0:256], in1=f2[:, 0:256],
                                                   op=ALU.subtract), reads=[ptotk, f2k], writes=[f3k])
            f4, f4k = scr(Fs, "F")
            S.add("act", lambda e: e.activation(out=f4[:, 0:256], in_=f3[:, 0:256], func=AF.Exp),
                  reads=[f3k], writes=[f4k])
            S.add("dve", lambda e: e.tensor_tensor(out=dtdte, in0=dt_tm, in1=f4[:, 0:256], op=ALU.mult),
                  reads=[dt_k, f4k], writes=[dtdte_k])

            cA, cAk = bigcell(16, 0)
            cB0, cB0k = bigcell(16, 1)
            dtaU, dtaUk = bigcell(17, 0)
            DT, DTk = bigcell(17, 1)
            MT, MTk = bigcell(18, 0)
            cF, t1k = bigcell(18, 1)
            cG, ytk = bigcell(19, 0)
            cH, thzk = bigcell(19, 1)
            cI, zs20k = bigcell(20, 0)
            cJ, yzk = bigcell(20, 1)
            cK, junkk = bigcell(21, 0)
            cL, cLk = bigcell(21, 1)
            xs_tm = cA[:, 0:256]
            xdt = cA[:, 256:512]
            cBs = [(cB0, cB0k), (hand2[:, 0:512], "hand2a")]
            zs2s = [(cI.bitcast(F32), zs20k), (hand2[:, 512:1024].bitcast(F32), "hand2b")]
            t1 = cF.bitcast(F32)
            yt = cG.bitcast(F32)
            thz = cH.bitcast(F32)
            yz = cJ.bitcast(F32)
            junk = cK.bitcast(F32)
            yn = cL[:, 0:256]
            S_bf = cL[:, 256:512]
            h4 = lambda ap: ap.rearrange("p (r x) -> p r x", r=4)

            slab_of = {}

            def load_slab(g):
                sl, slk, sld = slot()
                slv = sl[:, :].rearrange("p (k c) -> p k c", k=KD)
                for (c0, w, s0) in ((0, 256, g * 256), (256, 256, 2048 + g * 256),
                                    (512, 128, 4096 + g * 128), (640, 128, 5120 + g * 128)):
                    src = w_in[:, s0:s0 + w].rearrange("(k p) c -> p k c", p=128)
                    S.add("pool", lambda e, c0=c0, w=w, src=src, slv=slv: e.dma_start(
                        out=slv[:, :, c0:c0 + w], in_=src), writes=[slk], dsem=sld)
                slab_of[g] = (slv, slk)

            def bank(ix):
                return psb[ix], ("ps", ix)

            def inproj_gen(g):
                slv, slk = slab_of[g]
                xbc_g = xbc_sets[g % 2]
                xbck = xbck_sets[g % 2]
                dI, dIk = dIs[g % 2]
                dIb = dI.bitcast(BF16)
                for r in range(4):
                    dcol = pcol("m_d", j * 32 + 4 * g + r)
                    S.add("dve", lambda e, r=r, dcol=dcol, dIb=dIb: e.tensor_scalar(
                        out=dIb[:, r * 128:(r + 1) * 128], in0=ident_bf[:], scalar1=dcol, scalar2=None,
                        op0=ALU.mult), reads=["consts", "pk"], writes=[dIk])
                yield
                for q in range(4):
                    mq = (2 * g, 2 * g + 1, 16 + g, 24 + g)[q]
                    cv, cvk = scr(cvb, "cvb")
                    cvh = cv[:].bitcast(BF16)
                    ho = (j * 32 + mq) * 4
                    S.add("dve", lambda e, cvh=cvh, ho=ho: e.tensor_copy(out=cvh[:, HC - 4:HC], in_=halo_m[:, ho:ho + 4]),
                          reads=[("halo_m", j)], writes=[(cvk, "h")])
                    wh = [mcwh[:, (j * 4 + kk) * 32 + mq:(j * 4 + kk) * 32 + mq + 1] for kk in range(4)]
                    bh = mcwh[:, 256 + j * 32 + mq:256 + j * 32 + mq + 1]
                    dset = nxt("dgset", 2)
                    dgt = rstd_sb[dset][:].bitcast(BF16)
                    dgk = ("rstd", dset)
                    for kk in range(4):
                        S.add("dve", lambda e, dgt=dgt, kk=kk, wh=wh: e.tensor_scalar(
                            out=dgt[:, kk * 128:(kk + 1) * 128], in0=ident_bf[:], scalar1=wh[kk], scalar2=None,
                            op0=ALU.mult), reads=["consts", "consts2"], writes=[dgk])
                    yield
                    for tt in range(NT):
                        tsl = slice(tt * 512, (tt + 1) * 512)
                        pw, pwk = bank(6)
                        for k in range(KD):
                            S.add("pe", lambda e, pw=pw, slv=slv, q=q, k=k, tsl=tsl: e.matmul(
                                pw[:], lhsT=slv[:, k, 256 + q * 128:256 + (q + 1) * 128], rhs=hbuf[:, k, tsl],
                                start=(k == 0), stop=(k == KD - 1)),
                                reads=[slk, ("h", k, tt)], writes=[pwk])
                            if k % 2 == 1:
                                yield
                        S.add("act", lambda e, cvh=cvh, pw=pw, tt=tt: e.activation(
                            out=cvh[:, HC + tt * 512:HC + (tt + 1) * 512], in_=pw[:], func=AF.Copy),
                            reads=[pwk], writes=[(cvk, tt)])
                        yield
                        rk = [(cvk, tt), (cvk, "h") if tt == 0 else (cvk, tt - 1)]
                        pc, pck = bank(7)
                        for kk in range(4):
                            S.add("pe", lambda e, pc=pc, dgt=dgt, kk=kk, cvh=cvh, tt=tt: e.matmul(
                                pc[:], lhsT=dgt[:, kk * 128:(kk + 1) * 128],
                                rhs=cvh[:, HC - 3 + kk + tt * 512:HC - 3 + kk + tt * 512 + 512],
                                start=(kk == 0), stop=(kk == 3)), reads=rk + [dgk], writes=[pck])
                        yield
                        hx, hxk = scr(Fs, "F")
                        S.add("dve", lambda e, hx=hx, pc=pc, bh=bh: e.tensor_scalar(
                            out=hx[:], in0=pc[:], scalar1=bh, scalar2=None, op0=ALU.add),
                            reads=[pck, "consts2"], writes=[hxk])
                        yield
                        th, thk = scr(Fs, "F")
                        S.add("act", lambda e, th=th, hx=hx: e.activation(out=th[:], in_=hx[:], func=AF.Tanh),
                              reads=[hxk], writes=[thk])
                        yield
                        S.add("dve", lambda e, th=th, hx=hx, q=q, tsl=tsl, xbc_g=xbc_g: e.scalar_tensor_tensor(
                            out=xbc_g[q][:, tsl], in0=th[:], scalar=1.0, in1=hx[:], op0=ALU.add, op1=ALU.mult),
                            reads=[thk, hxk], writes=[xbck[q]])
                        yield
                    S.add("dve", lambda e, cvh=cvh, ho=ho: e.tensor_copy(
                        out=halo_m[:, ho:ho + 4], in_=cvh[:, HC + TU - 4:HC + TU]),
                        reads=[(cvk, NT - 1)], writes=[("halo_m", j)])
                    yield

            def scanA_gen(g, c, hs):
                slv, slk = slab_of[g]
                xbc_g = xbc_sets[g % 2]
                xbck = xbck_sets[g % 2]
                dI, dIk = dIs[g % 2]
                dIb = dI.bitcast(BF16)
                csl = slice(c * 128, (c + 1) * 128)
                hsl = slice(c * 32 + 4 * g, c * 32 + 4 * g + 4)
                cB, cBk = cBs[c % 2]
                zs2, zs2k = zs2s[c % 2]
                xdte = cB[:, 0:256]
                B_tm = cB[:, 256:384]
                CBm = cB[:, 384:512]
                ptr, ptrk = bank(0)
                ptb = ptr[:].bitcast(BF16)
                for q in range(3):
                    S.add("pe", lambda e, q=q, ptb=ptb, csl=csl: e.transpose(
                        out=ptb[:, q * 128:(q + 1) * 128], in_=xbc_g[q][:, csl], identity=ident_bf[:]),
                        reads=[xbck[q], "consts"], writes=[ptrk])
                pcb, pcbk = ptr[:, 256:512], ptrk
                S.add("pe", lambda e, pcb=pcb, csl=csl: e.matmul(
                    pcb[:, 0:128], lhsT=xbc_g[2][:, csl], rhs=xbc_g[3][:, csl], start=True, stop=True),
                    reads=[xbck[2], xbck[3]], writes=[pcbk])
                yield
                S.add("pool", lambda e, hsl=hsl: e.tensor_tensor(
                    out=h4(dtaU), in0=U_bf[:].unsqueeze(1).to_broadcast([128, 4, 128]),
                    in1=dta[:, hsl].unsqueeze(2).to_broadcast([128, 4, 128]), op=ALU.mult),
                    reads=[dta_k, "consts"], writes=[dtaUk])
                yield
                pseg, psegk = bank(1)
                for r in range(4):
                    S.add("pe", lambda e, pseg=pseg, r=r: e.matmul(
                        pseg[:, r * 128:(r + 1) * 128], lhsT=Mgt_bf[:], rhs=dtaU[:, r * 128:(r + 1) * 128],
                        start=True, stop=True), reads=[dtaUk, "consts"], writes=[psegk])
                yield
                S.add("act", lambda e, ptb=ptb: e.activation(out=xs_tm, in_=ptb[:, 0:256], func=AF.Copy),
                      reads=[ptrk], writes=[cAk])
                yield
                S.add("act", lambda e, ptb=ptb: e.activation(out=B_tm, in_=ptb[:, 256:384], func=AF.Copy),
                      reads=[ptrk], writes=[cBk])
                yield
                S.add("dve", lambda e, ptb=ptb, hsl=hsl: e.tensor_tensor(
                    out=h4(xdt), in0=h4(ptb[:, 0:256]), in1=dt_tm[:, hsl].unsqueeze(2).to_broadcast([128, 4, 64]),
                    op=ALU.mult), reads=[ptrk, dt_k], writes=[cAk])
                yield
                S.add("dve", lambda e, ptb=ptb, hsl=hsl: e.tensor_tensor(
                    out=h4(xdte), in0=h4(ptb[:, 0:256]), in1=dtdte[:, hsl].unsqueeze(2).to_broadcast([128, 4, 64]),
                    op=ALU.mult), reads=[ptrk, dtdte_k], writes=[cBk])
                yield
                S.add("dve", lambda e, pcb=pcb: e.tensor_tensor(out=CBm, in0=pcb[:, 0:128], in1=U_f32[:], op=ALU.mult),
                      reads=[pcbk, "consts"], writes=[cBk])
                yield
                S.add("act", lambda e, pseg=pseg: e.activation(out=DT, in_=pseg[:], func=AF.Exp),
                      reads=[psegk], writes=[DTk])
                yield
                pz, pzk = bank(2)
                for k in range(KD):
                    S.add("pe", lambda e, pz=pz, k=k, csl=csl, slv=slv: e.matmul(
                        pz[:, 0:256], lhsT=hbuf[:, k, csl], rhs=slv[:, k, 0:256], start=(k == 0), stop=(k == KD - 1)),
                        reads=[slk, ("h", k, c // 4)], writes=[pzk])
                    if k % 4 == 3:
                        yield
                S.add("dve", lambda e: e.tensor_tensor(
                    out=h4(MT), in0=h4(DT), in1=CBm.unsqueeze(1).to_broadcast([128, 4, 128]), op=ALU.mult),
                    reads=[DTk, cBk], writes=[MTk])
                yield
                S.add("act", lambda e, pz=pz: e.activation(out=thz, in_=pz[:, 0:256], func=AF.Tanh, scale=0.5),
                      reads=[pzk], writes=[thzk])
                yield
                py, pyk = bank(3 + c % 2)
                for r in range(4):
                    S.add("pe", lambda e, py=py, r=r: e.matmul(
                        py[:, r * 64:(r + 1) * 64], lhsT=MT[:, r * 128:(r + 1) * 128], rhs=xdt[:, r * 64:(r + 1) * 64],
                        start=True, stop=False), reads=[MTk, cAk], writes=[pyk])
                    S.add("pe", lambda e, py=py, r=r, dIb=dIb: e.matmul(
                        py[:, r * 64:(r + 1) * 64], lhsT=dIb[:, r * 128:(r + 1) * 128], rhs=xs_tm[:, r * 64:(r + 1) * 64],
                        start=False, stop=True), reads=[dIk, cAk], writes=[pyk])
                yield
                S.add("dve", lambda e, pz=pz, zs2=zs2: e.scalar_tensor_tensor(
                    out=zs2, in0=thz, scalar=1.0, in1=pz[:, 0:256], op0=ALU.add, op1=ALU.mult),
                    reads=[thzk, pzk], writes=[zs2k])
                yield
                hs[c] = (py, pyk)

            def scanB_gen(g, c, hs):
                xbc_g = xbc_sets[g % 2]
                xbck = xbck_sets[g % 2]
                csl = slice(c * 128, (c + 1) * 128)
                hsl = slice(c * 32 + 4 * g, c * 32 + 4 * g + 4)
                cB, cBk = cBs[c % 2]
                zs2, zs2k = zs2s[c % 2]
                xdte = cB[:, 0:256]
                B_tm = cB[:, 256:384]
                py, pyk = hs[c]
                Sg = Sst[:, (j * 8 + g) * 256:(j * 8 + g + 1) * 256]
                Sk = ("S", j, g)
                S.add("act", lambda e, Sg=Sg: e.activation(out=S_bf, in_=Sg, func=AF.Copy),
                      reads=[Sk], writes=[cLk])
                yield
                S.add("pe", lambda e, py=py, csl=csl: e.matmul(
                    py[:, 256:512], lhsT=xbc_g[3][:, csl], rhs=S_bf, start=True, stop=True),
                    reads=[xbck[3], cLk], writes=[pyk])
                pst, pstk = bank(5)
                S.add("pe", lambda e, pst=pst: e.matmul(pst[:, 0:256], lhsT=B_tm, rhs=xdte, start=True, stop=True),
                      reads=[cBk], writes=[pstk])
                yield
                S.add("dve", lambda e, py=py, hsl=hsl: e.tensor_tensor(
                    out=h4(t1), in0=h4(py[:, 256:512]), in1=ecs[:, hsl].unsqueeze(2).to_broadcast([128, 4, 64]),
                    op=ALU.mult), reads=[pyk, ecs_k], writes=[t1k])
                yield
                S.add("dve", lambda e, py=py: e.tensor_tensor(out=yt, in0=py[:, 0:256], in1=t1, op=ALU.add),
                      reads=[pyk, t1k], writes=[ytk])
                yield
                S.add("dve", lambda e, zs2=zs2: e.tensor_tensor(out=yz, in0=yt, in1=zs2, op=ALU.mult),
                      reads=[ytk, zs2k], writes=[yzk])
                yield
                S.add("dve", lambda e, Sg=Sg, hsl=hsl: e.tensor_tensor(
                    out=h4(Sg), in0=h4(Sg), in1=cd[:, hsl].unsqueeze(2).to_broadcast([128, 4, 64]), op=ALU.mult),
                    reads=[Sk, cd_k], writes=[Sk])
                yield
                S.add("dve", lambda e, Sg=Sg, pst=pst: e.tensor_tensor(out=Sg, in0=pst[:, 0:256], in1=Sg, op=ALU.add),
                      reads=[Sk, pstk], writes=[Sk])
                yield
                si = nxt("ssum", 4)
                ss_ap = ssum[:, si:si + 1]
                vv_ap = ssum[:, 4 + si:5 + si]
                ssk = ("ssum", si)
                S.add("act", lambda e, ss_ap=ss_ap: e.activation(out=junk, in_=yz, func=AF.Square, scale=0.5,
                                                                 accum_out=ss_ap),
                      reads=[yzk], writes=[junkk, ssk])
                yield
                S.add("act", lambda e, ss_ap=ss_ap: e.activation(
                    out=ss_ap, in_=ss_ap, func=AF.Identity, scale=1.0 / 64.0, bias=eps4_col[:, 0:1]),
                    reads=[ssk, "consts"], writes=[ssk])
                yield
                S.add("pool", lambda e, ss_ap=ss_ap, vv_ap=vv_ap: e.tensor_tensor(
                    out=vv_ap, in0=ss_ap, in1=mhalf[:, 0:1], op=ALU.pow),
                    reads=[ssk, "mhalf"], writes=[("ssv", si)])
                yield
                S.add("act", lambda e, vv_ap=vv_ap: e.activation(out=yn, in_=yz, func=AF.Copy, scale=vv_ap),
                      reads=[yzk, ("ssv", si)], writes=[cLk])
                yield
                pt2, pt2k = bank(5)
                pt2b = pt2[:].bitcast(BF16)[:, 512:1024]
                for q in range(2):
                    S.add("pe", lambda e, q=q, pt2b=pt2b: e.transpose(
                        out=pt2b[:, q * 128:(q + 1) * 128], in_=yn[:, q * 128:(q + 1) * 128], identity=ident_bf[:]),
                        reads=[cLk, "consts"], writes=[pt2k])
                yield
                for q in range(2):
                    gcol = pcol("m_norm_g", j * 16 + 2 * g + q)
                    S.add("act", lambda e, q=q, pt2b=pt2b, gcol=gcol, csl=csl, g=g: e.activation(
                        out=big[:, 2 * g + q, csl], in_=pt2b[:, q * 128:(q + 1) * 128], func=AF.Copy, scale=gcol),
                        reads=[pt2k, "pk"], writes=[("big", 2 * g + q, c // 4)])
                    yield

            def scan_gen(g):
                hs = {}
                for _ in scanA_gen(g, 0, hs):
                    yield
                for c in range(NCH):
                    gb = scanB_gen(g, c, hs)
                    ga = scanA_gen(g, c + 1, hs) if c + 1 < NCH else None
                    done_a = ga is None
                    done_b = False
                    while not (done_a and done_b):
                        if not done_b:
                            try:
                                next(gb)
                            except StopIteration:
                                done_b = True
                            yield
                        if not done_a:
                            try:
                                next(ga)
                            except StopIteration:
                                done_a = True
                            yield

            load_slab(0)
            load_slab(1)
            for _ in inproj_gen(0):
                pass
            for g in range(8):
                if g + 2 < 8:
                    load_slab(g + 2)
                interleave([[scan_gen(g), 3], [inproj_gen(g + 1) if g + 1 < 8 else None, 1]])
            out_proj_post(b, i, 0, m_w_out_d[j], 16)


        for u in range(nunits):
            b = u // UPS
            uh = u % UPS
            tok0 = uh * TU
            for k in range(KD):
                S.add("sp", lambda e, k=k, b=b, tok0=tok0: e.dma_start(
                    out=xres[:, k, :], in_=x_d[b, k * 128:(k + 1) * 128, tok0:tok0 + TU]),
                    writes=[("xres", k, tt) for tt in range(NT)], dsem=dsem("xin%d" % k))
            for i in layers:
                prenorm(b, i, 0)
                if i % 2 == 0:
                    conv_mixer(b, i, uh == 0)
                else:
                    ssd_mixer(b, i, uh == 0)
                prenorm(b, i, 1)
                ffn(b, i)
            for k in range(KD):
                S.add("sp", lambda e, k=k, b=b, tok0=tok0: e.dma_start(
                    out=out_d[b, k * 128:(k + 1) * 128, tok0:tok0 + TU], in_=xres[:, k, :]),
                    reads=[("xres", k, tt) for tt in range(NT)], writes=[("outd", k)],
                    dsem=dsem("xout%d" % k))
        S.add("sp", None, reads=[("outd", k) for k in range(KD)])

        S.emit(nc, sems, dsems)
    return nc


_CACHE = {}


def make_in_maps(inp):
    x = np.asarray(inp["x"], np.float32)
    c = np.asarray(inp["c"], np.float32)
    pk = pack_params(inp)
    shared = {
        "pk": pk,
        "ada_w": np.ascontiguousarray(inp["ada_w"], np.float32),
        "a_w_in": np.ascontiguousarray(inp["a_w_in"], np.float32),
        "a_w_out": np.ascontiguousarray(inp["a_w_out"], np.float32),
        "m_w_in": np.ascontiguousarray(inp["m_w_in"], np.float32),
        "m_w_out": np.ascontiguousarray(inp["m_w_out"], np.float32),
        "f_w_in": np.ascontiguousarray(inp["f_w_in"], np.float32),
        "f_w_out": np.ascontiguousarray(inp["f_w_out"], np.float32),
    }
    maps = []
    for r in range(NCORES):
        xs = x[r * NSEQ:(r + 1) * NSEQ]
        x_fm = np.ascontiguousarray(np.transpose(xs, (0, 2, 1)))
        cc = c[r * NSEQ:(r + 1) * NSEQ]
        cT = np.ascontiguousarray(
            np.transpose(cc.reshape(NSEQ, KD, 128), (2, 1, 0)).reshape(128, KD * NSEQ))
        m = dict(shared)
        m["x_fm"] = x_fm
        m["cT"] = cT
        maps.append(m)
    return maps


def kernel(**inputs):
    if "nc" not in _CACHE:
        _CACHE["nc"] = build_program()
    nc = _CACHE["nc"]
    maps = make_in_maps(inputs)
    res = run_bass_kernel_spmd(nc, maps, core_ids=list(range(NCORES)))
    outs = [np.transpose(np.asarray(r["out_fm"]), (0, 2, 1)) for r in res.results]
    return np.ascontiguousarray(np.concatenate(outs, axis=0).astype(np.float32))
```

```python
import numpy as np
import concourse.bass as bass
import concourse.mybir as mybir
from concourse.bass_utils import run_bass_kernel_spmd

F32 = mybir.dt.float32
BF16 = mybir.dt.bfloat16
AF = mybir.ActivationFunctionType
ALU = mybir.AluOpType

NCORES = 8
D = 1024
KD = 8
SEQ = 2048
DEPTH = 4
DFF = 2816
KF = 22
M_DI = 2048
M_IN = 6176
EPS = 1e-6
NSEQ = 2
NT = 2
TU = 512 * NT
UPS = SEQ // TU
NCH = TU // 128

PK = {}
_off = 0


def _pk(name, n):
    global _off
    PK[name] = _off
    _off += n


_pk("ada_b", 192)
_pk("norm_g", 128)
_pk("a_conv_w", 48)
_pk("m_conv_w", 256)
_pk("m_conv_b", 64)
_pk("m_norm_g", 32)
_pk("m_dt_bias", 64)
_pk("m_a_log", 64)
_pk("m_d", 64)
NPK = _off


def pack_params(inp):
    pk = np.zeros((128, NPK), np.float32)

    def fm(v):
        v = np.asarray(v, np.float32)
        lead = v.shape[:-1]
        n = v.shape[-1] // 128
        v = v.reshape(lead + (n, 128))
        v = np.moveaxis(v, -1, 0)
        return v.reshape(128, -1)

    pk[:, PK["ada_b"]:PK["ada_b"] + 192] = fm(inp["ada_b"])
    pk[:, PK["norm_g"]:PK["norm_g"] + 128] = fm(inp["norm_g"])
    pk[:, PK["a_conv_w"]:PK["a_conv_w"] + 48] = fm(inp["a_conv_w"])
    pk[:, PK["m_conv_w"]:PK["m_conv_w"] + 256] = fm(inp["m_conv_w"])
    pk[:, PK["m_conv_b"]:PK["m_conv_b"] + 64] = fm(inp["m_conv_b"])
    pk[:, PK["m_norm_g"]:PK["m_norm_g"] + 32] = fm(inp["m_norm_g"])
    for nm in ("m_dt_bias", "m_a_log", "m_d"):
        pk[:, PK[nm]:PK[nm] + 64] = np.broadcast_to(
            np.asarray(inp[nm], np.float32).reshape(1, 64), (128, 64))
    return pk


import os as _os
SAME_ENGINE_RAW_ONLY = _os.environ.get("K_FULLSYNC", "0") != "1"


class Op:
    __slots__ = ("eng", "fn", "deps", "is_dma", "dsem", "dval", "inc", "count", "idx")


class Sched:
    ENGS = ("pe", "act", "dve", "pool", "sp")

    def __init__(self):
        self.ops = []
        self.lastw = {}
        self.readers = {}
        self.dma_vals = {}

    def add(self, eng, fn, reads=(), writes=(), dsem=None):
        op = Op()
        op.eng = eng
        op.fn = fn
        op.is_dma = dsem is not None
        op.dsem = dsem
        op.inc = False
        op.count = 0
        op.idx = len(self.ops)
        if op.is_dma:
            v = self.dma_vals.get(dsem, 0) + 16
            self.dma_vals[dsem] = v
            op.dval = v
        else:
            op.dval = 0
        deps = set()
        raw = set()
        for k in reads:
            w = self.lastw.get(k)
            if w is not None:
                deps.add(w)
                raw.add(w)
        for k in writes:
            w = self.lastw.get(k)
            if w is not None:
                deps.add(w)
            for r in self.readers.get(k, ()):
                deps.add(r)
        deps.discard(op)
        fdeps = []
        for d in deps:
            if d.is_dma and op.is_dma and d.dsem == op.dsem:
                continue
            if (not d.is_dma) and (not op.is_dma) and d.eng == op.eng:
                if d.eng == "pe" or (SAME_ENGINE_RAW_ONLY and d not in raw):
                    continue
            fdeps.append(d)
        op.deps = fdeps
        for k in reads:
            self.readers.setdefault(k, []).append(op)
        for k in writes:
            self.lastw[k] = op
            self.readers[k] = []
        self.ops.append(op)
        return op

    def emit(self, nc, sems, dsems):
        for op in self.ops:
            for d in op.deps:
                if not d.is_dma:
                    d.inc = True
        cnt = {e: 0 for e in self.ENGS}
        for op in self.ops:
            if (not op.is_dma) and op.inc:
                cnt[op.eng] += 1
                op.count = cnt[op.eng]
        by_eng = {e: [] for e in self.ENGS}
        for op in self.ops:
            by_eng[op.eng].append(op)

        def run(eng_name, eng):
            seen = {}
            for op in by_eng[eng_name]:
                need = {}
                for d in op.deps:
                    if d.is_dma:
                        key = ("d", d.dsem)
                        val = d.dval
                    else:
                        key = ("e", d.eng)
                        val = d.count
                    if need.get(key, 0) < val:
                        need[key] = val
                for key, val in need.items():
                    if seen.get(key, 0) < val:
                        sem = dsems[key[1]] if key[0] == "d" else sems[key[1]]
                        eng.wait_ge(sem, val)
                        seen[key] = val
                if op.fn is None:
                    continue
                ins = op.fn(eng)
                if op.is_dma:
                    ins.then_inc(dsems[op.dsem], 16)
                elif op.inc:
                    ins.then_inc(sems[op.eng], 1)

        with nc.Block() as block:
            @block.tensor
            def _(e):
                run("pe", e)

            @block.scalar
            def _(e):
                run("act", e)

            @block.vector
            def _(e):
                run("dve", e)

            @block.gpsimd
            def _(e):
                run("pool", e)

            @block.sync
            def _(e):
                run("sp", e)


def build_program(layers=(0, 1, 2, 3), nunits=NSEQ * UPS, debug_out=None):
    nc = bass.Bass("TRN2", target_bir_lowering=False)
    S = Sched()

    def din(name, shape):
        return nc.dram_tensor(name, list(shape), F32, kind="ExternalInput").ap()

    x_d = din("x_fm", (NSEQ, D, SEQ))
    cT_d = din("cT", (128, KD * NSEQ))
    pk_d = din("pk", (128, NPK))
    ada_w_d = din("ada_w", (D, 24 * D))
    a_w_in_d = din("a_w_in", (2, D, 3 * D))
    a_w_out_d = din("a_w_out", (2, D, D))
    m_w_in_d = din("m_w_in", (2, D, M_IN))
    m_w_out_d = din("m_w_out", (2, M_DI, D))
    f_w_in_d = din("f_w_in", (DEPTH, D, 2 * DFF))
    f_w_out_d = din("f_w_out", (DEPTH, DFF, D))
    out_d = nc.dram_tensor("out_fm", [NSEQ, D, SEQ], F32, kind="ExternalOutput").ap()

    import contextlib
    ctx = contextlib.ExitStack()
    with ctx:
        def sb(name, shape, dt=F32):
            return ctx.enter_context(nc.sbuf_tensor(name, list(shape), dt))

        def ps(name, shape, dt=F32):
            return ctx.enter_context(nc.psum_tensor(name, list(shape), dt))

        sems = {e: ctx.enter_context(nc.semaphore("sem_" + e)) for e in ("pe", "act", "dve", "pool")}
        dsems = {}

        def dsem(name):
            if name not in dsems:
                dsems[name] = ctx.enter_context(nc.semaphore("d_" + name))
            return name

        HC = 4
        xres = sb("xres", (128, KD, TU), F32)
        hy = sb("hy", (128, KD * TU), F32)
        ysb = hy[:].rearrange("p (m t) -> p m t", m=KD)
        hy_bf = hy[:].bitcast(BF16)
        hbuf = hy_bf[:, 0:KD * TU].rearrange("p (k t) -> p k t", k=KD)
        big = sb("big", (128, KF, TU), BF16)

        def hkeys(k, tt):
            return [("h", k, tt), ("ysb", k // 2, k % 2)]

        def ykeys(m, tt):
            ks = [("ysb", m, tt)]
            if 2 * m + tt < KD:
                ks += [("h", 2 * m + tt, 0), ("h", 2 * m + tt, 1)]
            return ks

        xbc_sets = [[hy_bf[:, KD * TU + q * TU:KD * TU + (q + 1) * TU] for q in range(4)], None]
        xbck_sets = [[("ysb", 4 + q // 2, q % 2) for q in range(4)], [("xbc1", q) for q in range(4)]]

        def hyt(ti):
            o = 6 * TU + ti * 256
            return hy[:, o:o + 256], ("ysb", 6 + ti // 4, (ti // 2) % 2)

        def bigcell(k, tt):
            return big[:, k, tt * 512:(tt + 1) * 512], ("big", k, tt)

        NSLOT = 3
        SLOTW = 8 * 768
        slabs = [sb("slab%d" % i, (128, SLOTW), BF16) for i in range(NSLOT)]
        pk = sb("pk_sb", (128, NPK), F32)
        cT = sb("cT_sb", (128, KD * NSEQ), F32)
        cs_bf = sb("cs_bf", (128, KD * NSEQ), BF16)
        modraw = sb("modraw", (128, 192 * NSEQ), F32)
        modp = sb("modp", (128, NSEQ * DEPTH * 2 * 3 * 8), F32)
        ones_bf = sb("ones_bf", (128, 128), BF16)
        mhalf = sb("mhalf", (128, 8), F32)
        sq = [sb("sq%d" % i, (128, 512), BF16) for i in range(3)]
        Fs = [sb("F%d" % i, (128, 512), F32) for i in range(5)]
        tmpf = csb = ub = var_sb = Fs
        rstd_sb = [sb("rstd%d" % i, (128, 512), F32) for i in range(2)]
        cvb = [sb("cvb%d" % i, (128, HC + TU), F32) for i in range(2)]
        halo_a = sb("halo_a", (128, 2 * 8 * 2), F32)
        Sst = sb("Sst", (128, 2 * 8 * 256), F32)
        halo_m = sb("halo_m", (128, 2 * 32 * 4), F32)
        wdt = sb("wdt", (128, 2 * KD * 32), BF16)
        ident_bf = sb("ident_bf", (128, 128), BF16)
        U_f32 = sb("U_f32", (128, 128), F32)
        U_bf = sb("U_bf", (128, 128), BF16)
        Mgt_bf = sb("Mgt_bf", (128, 128), BF16)
        ones_f32 = sb("ones_f32", (128, 128), F32)
        aexp = sb("aexp", (128, 64), F32)
        mcwh = sb("mcwh", (128, 256 + 64), F32)
        ssum = sb("ssum", (128, 8), F32)
        xbc1 = sb("xbc1", (128, 4 * TU), BF16)
        xbc_sets[1] = [xbc1[:, q * TU:(q + 1) * TU] for q in range(4)]
        hand2 = sb("hand2", (128, 1024), BF16)
        eps_col = sb("eps_col", (128, 1), F32)
        eps4_col = sb("eps4_col", (128, 1), F32)
        psb = [ps("ps%d" % i, (128, 512), F32) for i in range(8)]

        rot = {}

        def nxt(name, n):
            i = rot.get(name, 0)
            rot[name] = i + 1
            return i % n

        def ps_work():
            i = nxt("psw", 6)
            return psb[i], ("ps", i)

        def ps_ss():
            i = 6 + nxt("pss", 2)
            return psb[i], ("ps", i)

        def scr(lst, name):
            i = nxt(name, len(lst))
            return lst[i], (name, i)

        def slot():
            i = nxt("slot", NSLOT)
            return slabs[i], ("slab", i), dsem("slab%d" % i)

        def pcol(name, idx):
            o = PK[name] + idx
            return pk[:, o:o + 1]

        def modcol(b, i, sub, kind, m):
            o = ((((b * DEPTH + i) * 2 + sub) * 3 + kind) * 8) + m
            return modp[:, o:o + 1]

        S.add("sp", lambda e: e.dma_start(out=pk[:], in_=pk_d[:, :]), writes=["pk"], dsem=dsem("pk"))
        S.add("sp", lambda e: e.dma_start(out=cT[:], in_=cT_d[:, :]), writes=["cT"], dsem=dsem("cT"))
        S.add("dve", lambda e: e.memset(ones_bf[:], 1.0 / 1024.0), writes=["ones"])
        S.add("dve", lambda e: e.memset(mhalf[:], -0.5), writes=["mhalf"])
        S.add("dve", lambda e: e.memset(halo_a[:], 0.0), writes=["halo_a"])
        S.add("act", lambda e: e.activation(out=cs_bf[:], in_=cT[:], func=AF.Silu),
              reads=["cT"], writes=["cs"])
        ACH = 6
        pmod, pmod_key = psb[7], ("ps", 7)
        for sidx in range(192 // ACH):
            sl, slk, sld = slot()
            slv = sl[:, 0:KD * ACH * 128].rearrange("p (k c) -> p k c", k=KD)
            src = ada_w_d[:, sidx * ACH * 128:(sidx + 1) * ACH * 128].rearrange("(k p) c -> p k c", p=128)
            S.add("pool", lambda e, slv=slv, src=src: e.dma_start(out=slv, in_=src),
                  writes=[slk], dsem=sld)
            for jj in range(ACH):
                j = sidx * ACH + jj
                for k in range(KD):
                    S.add("pe", lambda e, j=j, jj=jj, k=k, slv=slv: e.matmul(
                        pmod[:, j * NSEQ:(j + 1) * NSEQ], lhsT=slv[:, k, jj * 128:(jj + 1) * 128],
                        rhs=cs_bf[:, k * NSEQ:(k + 1) * NSEQ], start=(k == 0), stop=(k == KD - 1)),
                        reads=[slk, "cs"], writes=[pmod_key])
        ab = pk[:, PK["ada_b"]:PK["ada_b"] + 192]
        S.add("dve", lambda e: e.tensor_tensor(
            out=modraw[:].rearrange("p (j b) -> p j b", b=NSEQ),
            in0=pmod[:, 0:192 * NSEQ].rearrange("p (j b) -> p j b", b=NSEQ),
            in1=ab.unsqueeze(2).to_broadcast([128, 192, NSEQ]), op=ALU.add),
            reads=[pmod_key, "pk"], writes=["modraw"])
        mr = modraw[:].rearrange("p (j b) -> p j b", b=NSEQ)
        for b in range(NSEQ):
            for i in range(DEPTH):
                for sub in range(2):
                    j0 = ((i * 2 + sub) * 3) * 8
                    o = (((b * DEPTH + i) * 2 + sub) * 3) * 8
                    ng_pre = pk[:, PK["norm_g"] + (i * 4 + 2 * sub) * 8:PK["norm_g"] + (i * 4 + 2 * sub) * 8 + 8]
                    ng_post = pk[:, PK["norm_g"] + (i * 4 + 2 * sub + 1) * 8:PK["norm_g"] + (i * 4 + 2 * sub + 1) * 8 + 8]
                    S.add("dve", lambda e, o=o, j0=j0, b=b, ng_pre=ng_pre: e.scalar_tensor_tensor(
                        out=modp[:, o:o + 8], in0=mr[:, j0 + 8:j0 + 16, b], scalar=1.0, in1=ng_pre,
                        op0=ALU.add, op1=ALU.mult), reads=["modraw", "pk"], writes=["modp"])
                    S.add("dve", lambda e, o=o, j0=j0, b=b: e.tensor_copy(
                        out=modp[:, o + 8:o + 16], in_=mr[:, j0:j0 + 8, b]),
                        reads=["modraw"], writes=["modp"])
                    S.add("dve", lambda e, o=o, j0=j0, b=b, ng_post=ng_post: e.tensor_tensor(
                        out=modp[:, o + 16:o + 24], in0=mr[:, j0 + 16:j0 + 24, b], in1=ng_post,
                        op=ALU.mult), reads=["modraw", "pk"], writes=["modp"])

        dif = Fs[4]
        S.add("pool", lambda e: e.iota(dif[:, 0:128], pattern=[[1, 128]], base=0, channel_multiplier=-1,
                                       allow_small_or_imprecise_dtypes=True), writes=["dif"])
        S.add("dve", lambda e: e.tensor_scalar(out=U_f32[:], in0=dif[:, 0:128], scalar1=0.0, scalar2=None,
                                               op0=ALU.is_ge), reads=["dif"], writes=["consts"])
        S.add("dve", lambda e: e.tensor_scalar(out=U_bf[:], in0=dif[:, 0:128], scalar1=0.0, scalar2=None,
                                               op0=ALU.is_ge), reads=["dif"], writes=["consts"])
        S.add("dve", lambda e: e.tensor_scalar(out=Mgt_bf[:], in0=dif[:, 0:128], scalar1=0.0, scalar2=None,
                                               op0=ALU.is_lt), reads=["dif"], writes=["consts"])
        S.add("dve", lambda e: e.tensor_scalar(out=ident_bf[:], in0=dif[:, 0:128], scalar1=0.0, scalar2=None,
                                               op0=ALU.is_equal), reads=["dif"], writes=["consts"])
        S.add("dve", lambda e: e.memset(ones_f32[:], 1.0), writes=["consts"])
        S.add("dve", lambda e: e.memset(eps_col[:], EPS), writes=["consts"])
        S.add("dve", lambda e: e.memset(eps4_col[:], 4.0 * EPS), writes=["consts"])
        S.add("dve", lambda e: e.memset(Sst[:], 0.0), writes=[("S", jj, g) for jj in range(2) for g in range(8)])
        S.add("dve", lambda e: e.memset(halo_m[:], 0.0), writes=[("halo_m", 0), ("halo_m", 1)])
        S.add("act", lambda e: e.activation(out=aexp[:], in_=pk[:, PK["m_a_log"]:PK["m_a_log"] + 64], func=AF.Exp),
              reads=["pk"], writes=["consts2"])
        S.add("dve", lambda e: e.tensor_scalar(out=mcwh[:, 0:256], in0=pk[:, PK["m_conv_w"]:PK["m_conv_w"] + 256],
                                               scalar1=0.5, scalar2=None, op0=ALU.mult), reads=["pk"], writes=["consts2"])
        S.add("dve", lambda e: e.tensor_scalar(out=mcwh[:, 256:320], in0=pk[:, PK["m_conv_b"]:PK["m_conv_b"] + 64],
                                               scalar1=0.5, scalar2=None, op0=ALU.mult), reads=["pk"], writes=["consts2"])
        for jj in range(2):
            S.add("pool", lambda e, jj=jj: e.dma_start(
                out=wdt[:, jj * KD * 32:(jj + 1) * KD * 32].rearrange("p (k c) -> p k c", k=KD),
                in_=m_w_in_d[jj][:, 6144:6176].rearrange("(k p) c -> p k c", p=128)),
                writes=["wdt"], dsem=dsem("wdt"))

        def prenorm(b, i, sub):
            plist = []
            for tt in range(NT):
                tsl = slice(tt * 512, (tt + 1) * 512)
                pss, pssk = ps_ss()
                for k in range(KD):
                    s_, sk = scr(sq, "sq")
                    S.add("act", lambda e, s_=s_, k=k, tsl=tsl: e.activation(
                        out=s_[:], in_=xres[:, k, tsl], func=AF.Square),
                        reads=[("xres", k, tt)], writes=[sk])
                    S.add("pe", lambda e, s_=s_, k=k, pss=pss: e.matmul(
                        pss[:], lhsT=ones_bf[:], rhs=s_[:], start=(k == 0), stop=(k == KD - 1)),
                        reads=[sk, "ones"], writes=[pssk])
                plist.append((pss, pssk))
            rl = rstd_many(plist)
            for tt in range(NT):
                tsl = slice(tt * 512, (tt + 1) * 512)
                rs, rsk = rl[tt]
                for k in range(KD):
                    t_, tk = scr(Fs, "F")
                    S.add("dve", lambda e, t_=t_, k=k, tsl=tsl, rs=rs: e.tensor_tensor(
                        out=t_[:], in0=xres[:, k, tsl], in1=rs[:], op=ALU.mult),
                        reads=[("xres", k, tt), rsk], writes=[tk])
                    sc_ap = modcol(b, i, sub, 0, k)
                    bi_ap = modcol(b, i, sub, 1, k)
                    S.add("act", lambda e, t_=t_, k=k, tsl=tsl, sc_ap=sc_ap, bi_ap=bi_ap: e.activation(
                        out=hbuf[:, k, tsl], in_=t_[:], func=AF.Identity, scale=sc_ap, bias=bi_ap),
                        reads=[tk, "modp"], writes=hkeys(k, tt))

        def rstd_many(plist):
            vs = []
            for pss, pssk in plist:
                v_, vk = scr(Fs, "F")
                S.add("act", lambda e, v_=v_, pss=pss: e.activation(
                    out=v_[:], in_=pss[:], func=AF.Sqrt, bias=eps_col[:, 0:1], scale=1.0),
                    reads=[pssk, "consts"], writes=[vk])
                vs.append((v_, vk))
            outs = []
            for v_, vk in vs:
                rs, rsk = scr(rstd_sb, "rstd")
                S.add("dve", lambda e, v_=v_, rs=rs: e.reciprocal(out=rs[:], in_=v_[:]),
                      reads=[vk], writes=[rsk])
                outs.append((rs, rsk))
            return outs

        def out_proj_post(b, i, sub, w_d, nk):
            per = nk * 128
            pssl = [ps_ss() for tt in range(NT)]
            pend = []
            for m in range(KD):
                sl, slk, sld = slot()
                slv = sl[:, 0:per].rearrange("p (k c) -> p k c", k=nk)
                src = w_d[:, m * 128:(m + 1) * 128].rearrange("(k p) c -> p k c", p=128)
                S.add("pool", lambda e, slv=slv, src=src: e.dma_start(out=slv, in_=src),
                      writes=[slk], dsem=sld)
                for tt in range(NT):
                    tsl = slice(tt * 512, (tt + 1) * 512)
                    pss, pssk = pssl[tt]
                    pw, pwk = ps_work()
                    for k in range(nk):
                        S.add("pe", lambda e, pw=pw, slv=slv, k=k, tsl=tsl: e.matmul(
                            pw[:], lhsT=slv[:, k, :], rhs=big[:, k, tsl],
                            start=(k == 0), stop=(k == nk - 1)),
                            reads=[slk, ("big", k, tt)], writes=[pwk])
                    for f in pend:
                        f()
                    pend = []
                    S.add("act", lambda e, pw=pw, m=m, tsl=tsl: e.activation(
                        out=ysb[:, m, tsl], in_=pw[:], func=AF.Copy),
                        reads=[pwk], writes=ykeys(m, tt))
                    s_, sk = scr(sq, "sq")
                    S.add("act", lambda e, pw=pw, s_=s_: e.activation(
                        out=s_[:], in_=pw[:], func=AF.Square),
                        reads=[pwk], writes=[sk])

                    def ssmm(s_=s_, sk=sk, m=m, pss=pss, pssk=pssk):
                        S.add("pe", lambda e: e.matmul(
                            pss[:], lhsT=ones_bf[:], rhs=s_[:], start=(m == 0), stop=(m == KD - 1)),
                            reads=[sk, "ones"], writes=[pssk])
                    pend.append(ssmm)
            for f in pend:
                f()
            rl = rstd_many(pssl)
            for tt in range(NT):
                tsl = slice(tt * 512, (tt + 1) * 512)
                rs, rsk = rl[tt]
                for m in range(KD):
                    t_, tk = scr(Fs, "F")
                    gg_ap = modcol(b, i, sub, 2, m)
                    S.add("dve", lambda e, t_=t_, m=m, rs=rs, tsl=tsl, gg_ap=gg_ap: e.scalar_tensor_tensor(
                        out=t_[:], in0=ysb[:, m, tsl], scalar=gg_ap, in1=rs[:],
                        op0=ALU.mult, op1=ALU.mult),
                        reads=[("ysb", m, tt), rsk, "modp"], writes=[tk])
                    S.add("dve", lambda e, t_=t_, m=m, tsl=tsl: e.tensor_tensor(
                        out=xres[:, m, tsl], in0=xres[:, m, tsl], in1=t_[:], op=ALU.add),
                        reads=[tk, ("xres", m, tt)], writes=[("xres", m, tt)])

        def conv_mixer(b, i, first_unit):
            j = i // 2
            w_in = a_w_in_d[j]
            if first_unit:
                S.add("dve", lambda e: e.memset(halo_a[:, j * 16:(j + 1) * 16], 0.0),
                      reads=[], writes=[("halo_a", j)])
            for m in range(KD):
                sl, slk, sld = slot()
                slv = sl[:, 0:KD * 3 * 128].rearrange("p (k t c) -> p k t c", k=KD, t=3)
                for t3 in range(3):
                    src = w_in[:, (t3 * 8 + m) * 128:(t3 * 8 + m + 1) * 128].rearrange("(k p) c -> p k c", p=128)
                    S.add("pool", lambda e, slv=slv, t3=t3, src=src: e.dma_start(
                        out=slv[:, :, t3, :], in_=src), writes=[slk], dsem=sld)
                cv, cvk = scr(cvb, "cvb")
                ho = (j * 8 + m) * 2
                S.add("dve", lambda e, cv=cv, ho=ho: e.tensor_copy(out=cv[:, HC - 2:HC], in_=halo_a[:, ho:ho + 2]),
                      reads=[("halo_a", j)], writes=[(cvk, "h")])
                w0, w1, w2 = [pcol("a_conv_w", (j * 3 + kk) * 8 + m) for kk in range(3)]
                for tt in range(NT):
                    tsl = slice(tt * 512, (tt + 1) * 512)
                    pp = []
                    for t3 in range(3):
                        pw, pwk = ps_work()
                        for k in range(KD):
                            S.add("pe", lambda e, pw=pw, slv=slv, t3=t3, k=k, tsl=tsl: e.matmul(
                                pw[:], lhsT=slv[:, k, t3, :], rhs=hbuf[:, k, tsl],
                                start=(k == 0), stop=(k == KD - 1)),
                                reads=[slk, ("h", k, tt)], writes=[pwk])
                        pp.append((pw, pwk))
                    (pb, pbk), (pc, pck), (pv, pvk) = pp
                    c_, ck = scr(Fs, "F")
                    S.add("act", lambda e, c_=c_, pc=pc: e.activation(out=c_[:], in_=pc[:], func=AF.Copy),
                          reads=[pck], writes=[ck])
                    S.add("dve", lambda e, cv=cv, pv=pv, c_=c_, tt=tt: e.tensor_tensor(
                        out=cv[:, HC + tt * 512:HC + (tt + 1) * 512], in0=pv[:], in1=c_[:], op=ALU.mult),
                        reads=[pvk, ck], writes=[(cvk, tt)])
                    u_, uk = scr(Fs, "F")
                    rk = [(cvk, tt), (cvk, "h") if tt == 0 else (cvk, tt - 1)]
                    S.add("dve", lambda e, u_=u_, cv=cv, tt=tt, w0=w0: e.tensor_scalar(
                        out=u_[:], in0=cv[:, HC - 2 + tt * 512:HC - 2 + tt * 512 + 512], scalar1=w0, scalar2=None, op0=ALU.mult),
                        reads=rk + ["pk"], writes=[uk])
                    S.add("dve", lambda e, u_=u_, cv=cv, tt=tt, w1=w1: e.scalar_tensor_tensor(
                        out=u_[:], in0=cv[:, HC - 1 + tt * 512:HC - 1 + tt * 512 + 512], scalar=w1, in1=u_[:],
                        op0=ALU.mult, op1=ALU.add), reads=rk + [uk, "pk"], writes=[uk])
                    S.add("dve", lambda e, u_=u_, cv=cv, tt=tt, w2=w2: e.scalar_tensor_tensor(
                        out=u_[:], in0=cv[:, HC + tt * 512:HC + tt * 512 + 512], scalar=w2, in1=u_[:],
                        op0=ALU.mult, op1=ALU.add), reads=rk + [uk, "pk"], writes=[uk])
                    S.add("dve", lambda e, u_=u_, pb=pb, tsl=tsl, m=m: e.tensor_tensor(
                        out=big[:, m, tsl], in0=pb[:], in1=u_[:], op=ALU.mult),
                        reads=[pbk, uk], writes=[("big", m, tt)])
                S.add("dve", lambda e, cv=cv, ho=ho: e.tensor_copy(out=halo_a[:, ho:ho + 2], in_=cv[:, HC + TU - 2:HC + TU]),
                      reads=[(cvk, NT - 1)], writes=[("halo_a", j)])
            out_proj_post(b, i, 0, a_w_out_d[j], KD)

        def ffn(b, i):
            w_in = f_w_in_d[i]
            for m in range(KF):
                sl, slk, sld = slot()
                slv = sl[:, 0:KD * 2 * 128].rearrange("p (k t c) -> p k t c", k=KD, t=2)
                for t2 in range(2):
                    src = w_in[:, (t2 * KF + m) * 128:(t2 * KF + m + 1) * 128].rearrange("(k p) c -> p k c", p=128)
                    S.add("pool", lambda e, slv=slv, t2=t2, src=src: e.dma_start(
                        out=slv[:, :, t2, :], in_=src), writes=[slk], dsem=sld)
                for tt in range(NT):
                    tsl = slice(tt * 512, (tt + 1) * 512)
                    pp = []
                    for t2 in range(2):
                        pw, pwk = ps_work()
                        for k in range(KD):
                            S.add("pe", lambda e, pw=pw, slv=slv, t2=t2, k=k, tsl=tsl: e.matmul(
                                pw[:], lhsT=slv[:, k, t2, :], rhs=hbuf[:, k, tsl],
                                start=(k == 0), stop=(k == KD - 1)),
                                reads=[slk, ("h", k, tt)], writes=[pwk])
                        pp.append((pw, pwk))
                    (pg, pgk), (pu, puk) = pp
                    c_, ck = scr(Fs, "F")
                    S.add("act", lambda e, c_=c_, pg=pg: e.activation(out=c_[:], in_=pg[:], func=AF.Silu),
                          reads=[pgk], writes=[ck])
                    S.add("dve", lambda e, pu=pu, c_=c_, m=m, tsl=tsl: e.tensor_tensor(
                        out=big[:, m, tsl], in0=pu[:], in1=c_[:], op=ALU.mult),
                        reads=[puk, ck], writes=[("big", m, tt)])
            out_proj_post(b, i, 1, f_w_out_d[i], KF)

        def interleave(items):
            act_ = [list(it) for it in items if it[0] is not None]
            while act_:
                for it in list(act_):
                    for _ in range(it[1]):
                        try:
                            next(it[0])
                        except StopIteration:
                            act_.remove(it)
                            break

        def ssd_mixer(b, i, first_unit):
            j = i // 2
            w_in = m_w_in_d[j]
            if first_unit:
                S.add("dve", lambda e: e.memset(Sst[:, j * 2048:(j + 1) * 2048], 0.0),
                      writes=[("S", j, g) for g in range(8)])
                S.add("dve", lambda e: e.memset(halo_m[:, j * 128:(j + 1) * 128], 0.0), writes=[("halo_m", j)])
            dt_tm, dt_k = hyt(0)
            dta, dta_k = hyt(1)
            ecs, ecs_k = hyt(2)
            cd, cd_k = hyt(3)
            dtdte, dtdte_k = hyt(4)
            dIs = [hyt(6), hyt(7)]
            v3 = lambda ap: ap.rearrange("p (c h) -> p c h", h=32)
            pdt, pdtk = ps_work()
            for c in range(NCH):
                for k in range(KD):
                    S.add("pe", lambda e, c=c, k=k: e.matmul(
                        pdt[:, c * 32:(c + 1) * 32], lhsT=hbuf[:, k, c * 128:(c + 1) * 128],
                        rhs=wdt[:, (j * KD + k) * 32:(j * KD + k + 1) * 32], start=(k == 0), stop=(k == KD - 1)),
                        reads=[("h", k, c // 4), "wdt"], writes=[pdtk])
            f0, f0k = scr(Fs, "F")
            dtb = pk[:, PK["m_dt_bias"] + j * 32:PK["m_dt_bias"] + (j + 1) * 32]
            S.add("dve", lambda e: e.tensor_tensor(
                out=v3(f0[:, 0:256]), in0=v3(pdt[:, 0:256]), in1=dtb.unsqueeze(1).to_broadcast([128, NCH, 32]),
                op=ALU.add), reads=[pdtk, "pk"], writes=[f0k])
            f1, f1k = scr(Fs, "F")
            S.add("act", lambda e: e.activation(out=f1[:, 0:256], in_=f0[:, 0:256], func=AF.Exp),
                  reads=[f0k], writes=[f1k])
            S.add("act", lambda e: e.activation(out=dt_tm, in_=f1[:, 0:256], func=AF.Ln, bias=1.0, scale=1.0),
                  reads=[f1k], writes=[dt_k])
            ae = aexp[:, j * 32:(j + 1) * 32]
            S.add("dve", lambda e: e.scalar_tensor_tensor(
                out=v3(dta), in0=v3(dt_tm), scalar=-1.0, in1=ae.unsqueeze(1).to_broadcast([128, NCH, 32]),
                op0=ALU.mult, op1=ALU.mult), reads=[dt_k, "consts2"], writes=[dta_k])
            pcs, pcsk = ps_work()
            ptot, ptotk = ps_work()
            for c in range(NCH):
                S.add("pe", lambda e, c=c: e.matmul(pcs[:, c * 32:(c + 1) * 32], lhsT=U_f32[:],
                                                    rhs=dta[:, c * 32:(c + 1) * 32], start=True, stop=True),
                      reads=[dta_k, "consts"], writes=[pcsk])
                S.add("pe", lambda e, c=c: e.matmul(ptot[:, c * 32:(c + 1) * 32], lhsT=ones_f32[:],
                                                    rhs=dta[:, c * 32:(c + 1) * 32], start=True, stop=True),
                      reads=[dta_k, "consts"], writes=[ptotk])
            S.add("act", lambda e: e.activation(out=ecs, in_=pcs[:, 0:256], func=AF.Exp), reads=[pcsk], writes=[ecs_k])
            S.add("act", lambda e: e.activation(out=cd, in_=ptot[:, 0:256], func=AF.Exp), reads=[ptotk], writes=[cd_k])
            f2, f2k = scr(Fs, "F")
            S.add("act", lambda e: e.activation(out=f2[:, 0:256], in_=pcs[:, 0:256], func=AF.Copy),
                  reads=[pcsk], writes=[f2k])
            f3, f3k = scr(Fs, "F")
            S.add("dve", lambda e: e.tensor_tensor(out=f3[:, 0:256], in0=ptot[:, 0:256], in1=f2[:, 0:256],
                                                   op=ALU.subtract), reads=[ptotk, f2k], writes=[f3k])
            f4, f4k = scr(Fs, "F")
            S.add("act", lambda e: e.activation(out=f4[:, 0:256], in_=f3[:, 0:256], func=AF.Exp),
                  reads=[f3k], writes=[f4k])
            S.add("dve", lambda e: e.tensor_tensor(out=dtdte, in0=dt_tm, in1=f4[:, 0:256], op=ALU.mult),
                  reads=[dt_k, f4k], writes=[dtdte_k])

            cA, cAk = bigcell(16, 0)
            cB0, cB0k = bigcell(16, 1)
            dtaU, dtaUk = bigcell(17, 0)
            DT, DTk = bigcell(17, 1)
            MT, MTk = bigcell(18, 0)
            cF, t1k = bigcell(18, 1)
            cG, ytk = bigcell(19, 0)
            cH, thzk = bigcell(19, 1)
            cI, zs20k = bigcell(20, 0)
            cJ, yzk = bigcell(20, 1)
            cK, junkk = bigcell(21, 0)
            cL, cLk = bigcell(21, 1)
            xs_tm = cA[:, 0:256]
            xdt = cA[:, 256:512]
            cBs = [(cB0, cB0k), (hand2[:, 0:512], "hand2a")]
            zs2s = [(cI.bitcast(F32), zs20k), (hand2[:, 512:1024].bitcast(F32), "hand2b")]
            t1 = cF.bitcast(F32)
            yt = cG.bitcast(F32)
            thz = cH.bitcast(F32)
            yz = cJ.bitcast(F32)
            junk = cK.bitcast(F32)
            yn = cL[:, 0:256]
            S_bf = cL[:, 256:512]
            h4 = lambda ap: ap.rearrange("p (r x) -> p r x", r=4)

            slab_of = {}

            def load_slab(g):
                sl, slk, sld = slot()
                slv = sl[:, :].rearrange("p (k c) -> p k c", k=KD)
                for (c0, w, s0) in ((0, 256, g * 256), (256, 256, 2048 + g * 256),
                                    (512, 128, 4096 + g * 128), (640, 128, 5120 + g * 128)):
                    src = w_in[:, s0:s0 + w].rearrange("(k p) c -> p k c", p=128)
                    S.add("pool", lambda e, c0=c0, w=w, src=src, slv=slv: e.dma_start(
                        out=slv[:, :, c0:c0 + w], in_=src), writes=[slk], dsem=sld)
                slab_of[g] = (slv, slk)

            def bank(ix):
                return psb[ix], ("ps", ix)

            def inproj_gen(g):
                slv, slk = slab_of[g]
                xbc_g = xbc_sets[g % 2]
                xbck = xbck_sets[g % 2]
                dI, dIk = dIs[g % 2]
                dIb = dI.bitcast(BF16)
                for r in range(4):
                    dcol = pcol("m_d", j * 32 + 4 * g + r)
                    S.add("dve", lambda e, r=r, dcol=dcol, dIb=dIb: e.tensor_scalar(
                        out=dIb[:, r * 128:(r + 1) * 128], in0=ident_bf[:], scalar1=dcol, scalar2=None,
                        op0=ALU.mult), reads=["consts", "pk"], writes=[dIk])
                yield
                for q in range(4):
                    mq = (2 * g, 2 * g + 1, 16 + g, 24 + g)[q]
                    cv, cvk = scr(cvb, "cvb")
                    cvh = cv[:].bitcast(BF16)
                    ho = (j * 32 + mq) * 4
                    S.add("dve", lambda e, cvh=cvh, ho=ho: e.tensor_copy(out=cvh[:, HC - 4:HC], in_=halo_m[:, ho:ho + 4]),
                          reads=[("halo_m", j)], writes=[(cvk, "h")])
                    wh = [mcwh[:, (j * 4 + kk) * 32 + mq:(j * 4 + kk) * 32 + mq + 1] for kk in range(4)]
                    bh = mcwh[:, 256 + j * 32 + mq:256 + j * 32 + mq + 1]
                    dset = nxt("dgset", 2)
                    dgt = rstd_sb[dset][:].bitcast(BF16)
                    dgk = ("rstd", dset)
                    for kk in range(4):
                        S.add("dve", lambda e, dgt=dgt, kk=kk, wh=wh: e.tensor_scalar(
                            out=dgt[:, kk * 128:(kk + 1) * 128], in0=ident_bf[:], scalar1=wh[kk], scalar2=None,
                            op0=ALU.mult), reads=["consts", "consts2"], writes=[dgk])
                    yield
                    for tt in range(NT):
                        tsl = slice(tt * 512, (tt + 1) * 512)
                        pw, pwk = bank(6)
                        for k in range(KD):
                            S.add("pe", lambda e, pw=pw, slv=slv, q=q, k=k, tsl=tsl: e.matmul(
                                pw[:], lhsT=slv[:, k, 256 + q * 128:256 + (q + 1) * 128], rhs=hbuf[:, k, tsl],
                                start=(k == 0), stop=(k == KD - 1)),
                                reads=[slk, ("h", k, tt)], writes=[pwk])
                            if k % 2 == 1:
                                yield
                        S.add("act", lambda e, cvh=cvh, pw=pw, tt=tt: e.activation(
                            out=cvh[:, HC + tt * 512:HC + (tt + 1) * 512], in_=pw[:], func=AF.Copy),
                            reads=[pwk], writes=[(cvk, tt)])
                        yield
                        rk = [(cvk, tt), (cvk, "h") if tt == 0 else (cvk, tt - 1)]
                        pc, pck = bank(7)
                        for kk in range(4):
                            S.add("pe", lambda e, pc=pc, dgt=dgt, kk=kk, cvh=cvh, tt=tt: e.matmul(
                                pc[:], lhsT=dgt[:, kk * 128:(kk + 1) * 128],
                                rhs=cvh[:, HC - 3 + kk + tt * 512:HC - 3 + kk + tt * 512 + 512],
                                start=(kk == 0), stop=(kk == 3)), reads=rk + [dgk], writes=[pck])
                        yield
                        hx, hxk = scr(Fs, "F")
                        S.add("dve", lambda e, hx=hx, pc=pc, bh=bh: e.tensor_scalar(
                            out=hx[:], in0=pc[:], scalar1=bh, scalar2=None, op0=ALU.add),
                            reads=[pck, "consts2"], writes=[hxk])
                        yield
                        th, thk = scr(Fs, "F")
                        S.add("act", lambda e, th=th, hx=hx: e.activation(out=th[:], in_=hx[:], func=AF.Tanh),
                              reads=[hxk], writes=[thk])
                        yield
                        S.add("dve", lambda e, th=th, hx=hx, q=q, tsl=tsl, xbc_g=xbc_g: e.scalar_tensor_tensor(
                            out=xbc_g[q][:, tsl], in0=th[:], scalar=1.0, in1=hx[:], op0=ALU.add, op1=ALU.mult),
                            reads=[thk, hxk], writes=[xbck[q]])
                        yield
                    S.add("dve", lambda e, cvh=cvh, ho=ho: e.tensor_copy(
                        out=halo_m[:, ho:ho + 4], in_=cvh[:, HC + TU - 4:HC + TU]),
                        reads=[(cvk, NT - 1)], writes=[("halo_m", j)])
                    yield

            def scanA_gen(g, c, hs):
                slv, slk = slab_of[g]
                xbc_g = xbc_sets[g % 2]
                xbck = xbck_sets[g % 2]
                dI, dIk = dIs[g % 2]
                dIb = dI.bitcast(BF16)
                csl = slice(c * 128, (c + 1) * 128)
                hsl = slice(c * 32 + 4 * g, c * 32 + 4 * g + 4)
                cB, cBk = cBs[c % 2]
                zs2, zs2k = zs2s[c % 2]
                xdte = cB[:, 0:256]
                B_tm = cB[:, 256:384]
                CBm = cB[:, 384:512]
                ptr, ptrk = bank(0)
                ptb = ptr[:].bitcast(BF16)
                for q in range(3):
                    S.add("pe", lambda e, q=q, ptb=ptb, csl=csl: e.transpose(
                        out=ptb[:, q * 128:(q + 1) * 128], in_=xbc_g[q][:, csl], identity=ident_bf[:]),
                        reads=[xbck[q], "consts"], writes=[ptrk])
                pcb, pcbk = ptr[:, 256:512], ptrk
                S.add("pe", lambda e, pcb=pcb, csl=csl: e.matmul(
                    pcb[:, 0:128], lhsT=xbc_g[2][:, csl], rhs=xbc_g[3][:, csl], start=True, stop=True),
                    reads=[xbck[2], xbck[3]], writes=[pcbk])
                yield
                S.add("pool", lambda e, hsl=hsl: e.tensor_tensor(
                    out=h4(dtaU), in0=U_bf[:].unsqueeze(1).to_broadcast([128, 4, 128]),
                    in1=dta[:, hsl].unsqueeze(2).to_broadcast([128, 4, 128]), op=ALU.mult),
                    reads=[dta_k, "consts"], writes=[dtaUk])
                yield
                pseg, psegk = bank(1)
                for r in range(4):
                    S.add("pe", lambda e, pseg=pseg, r=r: e.matmul(
                        pseg[:, r * 128:(r + 1) * 128], lhsT=Mgt_bf[:], rhs=dtaU[:, r * 128:(r + 1) * 128],
                        start=True, stop=True), reads=[dtaUk, "consts"], writes=[psegk])
                yield
                S.add("act", lambda e, ptb=ptb: e.activation(out=xs_tm, in_=ptb[:, 0:256], func=AF.Copy),
                      reads=[ptrk], writes=[cAk])
                yield
                S.add("act", lambda e, ptb=ptb: e.activation(out=B_tm, in_=ptb[:, 256:384], func=AF.Copy),
                      reads=[ptrk], writes=[cBk])
                yield
                S.add("dve", lambda e, ptb=ptb, hsl=hsl: e.tensor_tensor(
                    out=h4(xdt), in0=h4(ptb[:, 0:256]), in1=dt_tm[:, hsl].unsqueeze(2).to_broadcast([128, 4, 64]),
                    op=ALU.mult), reads=[ptrk, dt_k], writes=[cAk])
                yield
                S.add("dve", lambda e, ptb=ptb, hsl=hsl: e.tensor_tensor(
                    out=h4(xdte), in0=h4(ptb[:, 0:256]), in1=dtdte[:, hsl].unsqueeze(2).to_broadcast([128, 4, 64]),
                    op=ALU.mult), reads=[ptrk, dtdte_k], writes=[cBk])
                yield
                S.add("dve", lambda e, pcb=pcb: e.tensor_tensor(out=CBm, in0=pcb[:, 0:128], in1=U_f32[:], op=ALU.mult),
                      reads=[pcbk, "consts"], writes=[cBk])
                yield
                S.add("act", lambda e, pseg=pseg: e.activation(out=DT, in_=pseg[:], func=AF.Exp),
                      reads=[psegk], writes=[DTk])
                yield
                pz, pzk = bank(2)
                for k in range(KD):
                    S.add("pe", lambda e, pz=pz, k=k, csl=csl, slv=slv: e.matmul(
                        pz[:, 0:256], lhsT=hbuf[:, k, csl], rhs=slv[:, k, 0:256], start=(k == 0), stop=(k == KD - 1)),
                        reads=[slk, ("h", k, c // 4)], writes=[pzk])
                    if k % 4 == 3:
                        yield
                S.add("dve", lambda e: e.tensor_tensor(
                    out=h4(MT), in0=h4(DT), in1=CBm.unsqueeze(1).to_broadcast([128, 4, 128]), op=ALU.mult),
                    reads=[DTk, cBk], writes=[MTk])
                yield
                S.add("act", lambda e, pz=pz: e.activation(out=thz, in_=pz[:, 0:256], func=AF.Tanh, scale=0.5),
                      reads=[pzk], writes=[thzk])
                yield
                py, pyk = bank(3 + c % 2)
                for r in range(4):
                    S.add("pe", lambda e, py=py, r=r: e.matmul(
                        py[:, r * 64:(r + 1) * 64], lhsT=MT[:, r * 128:(r + 1) * 128], rhs=xdt[:, r * 64:(r + 1) * 64],
                        start=True, stop=False), reads=[MTk, cAk], writes=[pyk])
                    S.add("pe", lambda e, py=py, r=r, dIb=dIb: e.matmul(
                        py[:, r * 64:(r + 1) * 64], lhsT=dIb[:, r * 128:(r + 1) * 128], rhs=xs_tm[:, r * 64:(r + 1) * 64],
                        start=False, stop=True), reads=[dIk, cAk], writes=[pyk])
                yield
                S.add("dve", lambda e, pz=pz, zs2=zs2: e.scalar_tensor_tensor(
                    out=zs2, in0=thz, scalar=1.0, in1=pz[:, 0:256], op0=ALU.add, op1=ALU.mult),
                    reads=[thzk, pzk], writes=[zs2k])
                yield
                hs[c] = (py, pyk)

            def scanB_gen(g, c, hs):
                xbc_g = xbc_sets[g % 2]
                xbck = xbck_sets[g % 2]
                csl = slice(c * 128, (c + 1) * 128)
                hsl = slice(c * 32 + 4 * g, c * 32 + 4 * g + 4)
                cB, cBk = cBs[c % 2]
                zs2, zs2k = zs2s[c % 2]
                xdte = cB[:, 0:256]
                B_tm = cB[:, 256:384]
                py, pyk = hs[c]
                Sg = Sst[:, (j * 8 + g) * 256:(j * 8 + g + 1) * 256]
                Sk = ("S", j, g)
                if c == 0:
                    S.add("dve", lambda e, Sg=Sg: e.tensor_copy(out=S_bf, in_=Sg),
                          reads=[Sk], writes=[(cLk, "sbf")])
                    yield
                S.add("pe", lambda e, py=py, csl=csl: e.matmul(
                    py[:, 256:512], lhsT=xbc_g[3][:, csl], rhs=S_bf, start=True, stop=True),
                    reads=[xbck[3], (cLk, "sbf")], writes=[pyk])
                pst, pstk = bank(5)
                S.add("pe", lambda e, pst=pst: e.matmul(pst[:, 0:256], lhsT=B_tm, rhs=xdte, start=True, stop=True),
                      reads=[cBk], writes=[pstk])
                yield
                S.add("dve", lambda e, py=py, hsl=hsl: e.tensor_tensor(
                    out=h4(t1), in0=h4(py[:, 256:512]), in1=ecs[:, hsl].unsqueeze(2).to_broadcast([128, 4, 64]),
                    op=ALU.mult), reads=[pyk, ecs_k], writes=[t1k])
                yield
                S.add("dve", lambda e, py=py: e.tensor_tensor(out=yt, in0=py[:, 0:256], in1=t1, op=ALU.add),
                      reads=[pyk, t1k], writes=[ytk])
                yield
                S.add("dve", lambda e, zs2=zs2: e.tensor_tensor(out=yz, in0=yt, in1=zs2, op=ALU.mult),
                      reads=[ytk, zs2k], writes=[yzk])
                yield
                S.add("dve", lambda e, Sg=Sg, hsl=hsl: e.tensor_tensor(
                    out=h4(Sg), in0=h4(Sg), in1=cd[:, hsl].unsqueeze(2).to_broadcast([128, 4, 64]), op=ALU.mult),
                    reads=[Sk, cd_k], writes=[Sk])
                yield
                S.add("dve", lambda e, Sg=Sg, pst=pst: e.tensor_tensor(out=Sg, in0=pst[:, 0:256], in1=Sg, op=ALU.add),
                      reads=[Sk, pstk], writes=[Sk])
                yield
                if c + 1 < NCH:
                    S.add("dve", lambda e, Sg=Sg: e.tensor_copy(out=S_bf, in_=Sg),
                          reads=[Sk], writes=[(cLk, "sbf")])
                    yield
                si = nxt("ssum", 4)
                ss_ap = ssum[:, si:si + 1]
                vv_ap = ssum[:, 4 + si:5 + si]
                ssk = ("ssum", si)
                S.add("act", lambda e, ss_ap=ss_ap: e.activation(out=junk, in_=yz, func=AF.Square, scale=0.5,
                                                                 accum_out=ss_ap),
                      reads=[yzk], writes=[junkk, ssk])
                yield
                S.add("act", lambda e, ss_ap=ss_ap: e.activation(
                    out=ss_ap, in_=ss_ap, func=AF.Identity, scale=1.0 / 64.0, bias=eps4_col[:, 0:1]),
                    reads=[ssk, "consts"], writes=[ssk])
                yield
                S.add("pool", lambda e, ss_ap=ss_ap, vv_ap=vv_ap: e.tensor_tensor(
                    out=vv_ap, in0=ss_ap, in1=mhalf[:, 0:1], op=ALU.pow),
                    reads=[ssk, "mhalf"], writes=[("ssv", si)])
                yield
                S.add("act", lambda e, vv_ap=vv_ap: e.activation(out=yn, in_=yz, func=AF.Copy, scale=vv_ap),
                      reads=[yzk, ("ssv", si)], writes=[(cLk, "yn")])
                yield
                pt2, pt2k = bank(5)
                pt2b = pt2[:].bitcast(BF16)[:, 512:1024]
                for q in range(2):
                    S.add("pe", lambda e, q=q, pt2b=pt2b: e.transpose(
                        out=pt2b[:, q * 128:(q + 1) * 128], in_=yn[:, q * 128:(q + 1) * 128], identity=ident_bf[:]),
                        reads=[(cLk, "yn"), "consts"], writes=[pt2k])
                yield
                for q in range(2):
                    gcol = pcol("m_norm_g", j * 16 + 2 * g + q)
                    S.add("act", lambda e, q=q, pt2b=pt2b, gcol=gcol, csl=csl, g=g: e.activation(
                        out=big[:, 2 * g + q, csl], in_=pt2b[:, q * 128:(q + 1) * 128], func=AF.Copy, scale=gcol),
                        reads=[pt2k, "pk"], writes=[("big", 2 * g + q, c // 4)])
                    yield

            def scan_gen(g):
                hs = {}
                for _ in scanA_gen(g, 0, hs):
                    yield
                for c in range(NCH):
                    gb = scanB_gen(g, c, hs)
                    ga = scanA_gen(g, c + 1, hs) if c + 1 < NCH else None
                    done_a = ga is None
                    done_b = False
                    while not (done_a and done_b):
                        if not done_b:
                            try:
                                next(gb)
                            except StopIteration:
                                done_b = True
                            yield
                        if not done_a:
                            try:
                                next(ga)
                            except StopIteration:
                                done_a = True
                            yield

            load_slab(0)
            load_slab(1)
            for _ in inproj_gen(0):
                pass
            for g in range(8):
                if g + 2 < 8:
                    load_slab(g + 2)
                interleave([[scan_gen(g), 3], [inproj_gen(g + 1) if g + 1 < 8 else None, 1]])
            out_proj_post(b, i, 0, m_w_out_d[j], 16)


        for u in range(nunits):
            b = u // UPS
            uh = u % UPS
            tok0 = uh * TU
            for k in range(KD):
                S.add("sp", lambda e, k=k, b=b, tok0=tok0: e.dma_start(
                    out=xres[:, k, :], in_=x_d[b, k * 128:(k + 1) * 128, tok0:tok0 + TU]),
                    writes=[("xres", k, tt) for tt in range(NT)], dsem=dsem("xin%d" % k))
            for i in layers:
                prenorm(b, i, 0)
                if i % 2 == 0:
                    conv_mixer(b, i, uh == 0)
                else:
                    ssd_mixer(b, i, uh == 0)
                prenorm(b, i, 1)
                ffn(b, i)
            for k in range(KD):
                S.add("sp", lambda e, k=k, b=b, tok0=tok0: e.dma_start(
                    out=out_d[b, k * 128:(k + 1) * 128, tok0:tok0 + TU], in_=xres[:, k, :]),
                    reads=[("xres", k, tt) for tt in range(NT)], writes=[("outd", k)],
                    dsem=dsem("xout%d" % k))
        S.add("sp", None, reads=[("outd", k) for k in range(KD)])

        S.emit(nc, sems, dsems)
    return nc


_CACHE = {}


def make_in_maps(inp):
    x = np.asarray(inp["x"], np.float32)
    c = np.asarray(inp["c"], np.float32)
    pk = pack_params(inp)
    shared = {
        "pk": pk,
        "ada_w": np.ascontiguousarray(inp["ada_w"], np.float32),
        "a_w_in": np.ascontiguousarray(inp["a_w_in"], np.float32),
        "a_w_out": np.ascontiguousarray(inp["a_w_out"], np.float32),
        "m_w_in": np.ascontiguousarray(inp["m_w_in"], np.float32),
        "m_w_out": np.ascontiguousarray(inp["m_w_out"], np.float32),
        "f_w_in": np.ascontiguousarray(inp["f_w_in"], np.float32),
        "f_w_out": np.ascontiguousarray(inp["f_w_out"], np.float32),
    }
    maps = []
    for r in range(NCORES):
        xs = x[r * NSEQ:(r + 1) * NSEQ]
        x_fm = np.ascontiguousarray(np.transpose(xs, (0, 2, 1)))
        cc = c[r * NSEQ:(r + 1) * NSEQ]
        cT = np.ascontiguousarray(
            np.transpose(cc.reshape(NSEQ, KD, 128), (2, 1, 0)).reshape(128, KD * NSEQ))
        m = dict(shared)
        m["x_fm"] = x_fm
        m["cT"] = cT
        maps.append(m)
    return maps


def kernel(**inputs):
    if "nc" not in _CACHE:
        _CACHE["nc"] = build_program()
    nc = _CACHE["nc"]
    maps = make_in_maps(inputs)
    res = run_bass_kernel_spmd(nc, maps, core_ids=list(range(NCORES)))
    outs = [np.transpose(np.asarray(r["out_fm"]), (0, 2, 1)) for r in res.results]
    return np.ascontiguousarray(np.concatenate(outs, axis=0).astype(np.float32))
```
